# Optimizing a Trainium2 kernel written in Bass

```python
import jax, jax.numpy as jnp
from jax import lax
import numpy as np

D_MODEL = 4096
BATCH = 2
SEQ = 8192
DEPTH = 4

GRID_W = 64
N_MIXERS = 2
Q_BLOCK = 128
ROPE_THETA = 10000.0
EPS = 1e-6
D_FF = 3 * D_MODEL // 2
HEAD_DIM = 128
TOK_W = 3 * D_MODEL // 4
A_HEADS = TOK_W // HEAD_DIM
A_NOPE = HEAD_DIM
A_ROPE = 64
A_V = HEAD_DIM
A_Q_LORA = D_MODEL // 4
A_KV_LORA = D_MODEL // 8
B_HEADS = TOK_W // HEAD_DIM
B_KV_HEADS = B_HEADS // 4
B_HD = HEAD_DIM
MEM_TOKENS = 256
MEM_HEADS = 4
MEM_HD = D_MODEL // (4 * MEM_HEADS)
MEM_W = MEM_HEADS * MEM_HD
MIX_W = TOK_W + MEM_W
A_IN = A_Q_LORA + A_KV_LORA + A_ROPE + MEM_W
B_IN = B_HEADS * B_HD + 2 * B_KV_HEADS * B_HD + MEM_W
N_A = (DEPTH + N_MIXERS - 1) // N_MIXERS
N_B = DEPTH // N_MIXERS

kernel_name = "hybrid_mla_axial_gqa_macaron_memory"


def rms_norm(x, g):
    xf = x.astype(jnp.float32)
    y = xf * lax.rsqrt(jnp.mean(xf * xf, axis=-1, keepdims=True) + EPS)
    return (y * g.astype(jnp.float32)).astype(x.dtype)


def swiglu(x, w_gu, w_down):
    g, up = jnp.split(x @ w_gu, 2, axis=-1)
    return (jax.nn.silu(g) * up) @ w_down


def axial_rope_tables(seq_len, rot_dim):
    rows = seq_len // GRID_W
    row, col = jnp.meshgrid(jnp.arange(rows), jnp.arange(GRID_W), indexing="ij")
    row = row.reshape(-1).astype(jnp.float32)
    col = col.reshape(-1).astype(jnp.float32)
    axis_dim = rot_dim // 2
    inv_freq = ROPE_THETA ** (-jnp.arange(0, axis_dim, 2, dtype=jnp.float32) / axis_dim)
    ang = jnp.concatenate([row[:, None] * inv_freq, col[:, None] * inv_freq], axis=-1)
    return jnp.cos(ang), jnp.sin(ang)


def apply_rope(x, cos, sin):
    xf = x.astype(jnp.float32).reshape(x.shape[:-1] + (x.shape[-1] // 2, 2))
    x0, x1 = xf[..., 0], xf[..., 1]
    out = jnp.stack([x0 * cos - x1 * sin, x0 * sin + x1 * cos], axis=-1)
    return out.reshape(x.shape).astype(x.dtype)


def mla_attention(q_nope, q_pe, k_nope, k_pe, v):
    b, s, h, _ = q_nope.shape
    nb = s // Q_BLOCK
    qn = q_nope.reshape(b, nb, Q_BLOCK, h, A_NOPE).transpose(1, 0, 2, 3, 4)
    qp = q_pe.reshape(b, nb, Q_BLOCK, h, A_ROPE).transpose(1, 0, 2, 3, 4)
    scale = (A_NOPE + A_ROPE) ** -0.5

    def block(args):
        qn_b, qp_b = args
        sc = jnp.einsum("bqhd,bkhd->bhqk", qn_b, k_nope) + jnp.einsum("bqhr,bkr->bhqk", qp_b, k_pe)
        p = jax.nn.softmax(sc.astype(jnp.float32) * scale, axis=-1).astype(v.dtype)
        return jnp.einsum("bhqk,bkhd->bqhd", p, v)

    o = lax.map(block, (qn, qp))
    return o.transpose(1, 0, 2, 3, 4).reshape(b, s, h * A_V)


def gqa_attention(q, k, v):
    b, s, h, d = q.shape
    kvh = k.shape[2]
    g = h // kvh
    nb = s // Q_BLOCK
    qb = q.reshape(b, nb, Q_BLOCK, kvh, g, d).transpose(1, 0, 2, 3, 4, 5)
    scale = d ** -0.5

    def block(q_b):
        sc = jnp.einsum("bqkgd,bskd->bkgqs", q_b, k)
        p = jax.nn.softmax(sc.astype(jnp.float32) * scale, axis=-1).astype(v.dtype)
        return jnp.einsum("bkgqs,bskd->bqkgd", p, v)

    o = lax.map(block, qb)
    return o.transpose(1, 0, 2, 3, 4, 5).reshape(b, s, h * d)


def memory_attention(q, k, v):
    b, s = q.shape[:2]
    sc = jnp.einsum("bshd,bmhd->bhsm", q, k)
    p = jax.nn.softmax(sc.astype(jnp.float32) * (MEM_HD ** -0.5), axis=-1).astype(v.dtype)
    return jnp.einsum("bhsm,bmhd->bshd", p, v).reshape(b, s, MEM_W)


def mla_mixer(u, w_in, q_a_norm, kv_a_norm, w_q_b, w_kv_b,
              q_nope_norm, q_pe_norm, k_nope_norm, k_pe_norm, cos, sin):
    b, s, _ = u.shape
    proj = u @ w_in
    c_q, c_kv, k_pe, q_mem = jnp.split(
        proj, [A_Q_LORA, A_Q_LORA + A_KV_LORA, A_Q_LORA + A_KV_LORA + A_ROPE], axis=-1)
    q = (rms_norm(c_q, q_a_norm) @ w_q_b).reshape(b, s, A_HEADS, A_NOPE + A_ROPE)
    q_nope = rms_norm(q[..., :A_NOPE], q_nope_norm)
    q_pe = apply_rope(rms_norm(q[..., A_NOPE:], q_pe_norm), cos[:, None, :], sin[:, None, :])
    kv = (rms_norm(c_kv, kv_a_norm) @ w_kv_b).reshape(b, s, A_HEADS, A_NOPE + A_V)
    k_nope = rms_norm(kv[..., :A_NOPE], k_nope_norm)
    v = kv[..., A_NOPE:]
    k_pe = apply_rope(rms_norm(k_pe, k_pe_norm), cos, sin)
    return mla_attention(q_nope, q_pe, k_nope, k_pe, v), q_mem


def gqa_mixer(u, w_in, q_norm, k_norm, cos, sin):
    b, s, _ = u.shape
    proj = u @ w_in
    qw, kw = B_HEADS * B_HD, B_KV_HEADS * B_HD
    q, k, v, q_mem = jnp.split(proj, [qw, qw + kw, qw + 2 * kw], axis=-1)
    q = apply_rope(rms_norm(q.reshape(b, s, B_HEADS, B_HD), q_norm), cos[:, None, :], sin[:, None, :])
    k = apply_rope(rms_norm(k.reshape(b, s, B_KV_HEADS, B_HD), k_norm), cos[:, None, :], sin[:, None, :])
    v = v.reshape(b, s, B_KV_HEADS, B_HD)
    return gqa_attention(q, k, v), q_mem


def setup_inputs(seed: int = 0) -> dict:
    key = jax.random.key(seed)
    ks = jax.random.split(key, 26)
    f32 = jnp.float32

    def w(k, shape, fan_in):
        return jax.random.normal(k, shape, f32) * (fan_in ** -0.5)

    def g(k, shape):
        return 1.0 + 0.01 * jax.random.normal(k, shape, f32)

    return {
        "x": jax.random.normal(ks[0], (BATCH, SEQ, D_MODEL), f32),
        "mem": jax.random.normal(ks[1], (BATCH, MEM_TOKENS, D_MODEL), f32),
        "ffn1_norm": g(ks[2], (DEPTH, D_MODEL)),
        "ffn1_w_gu": w(ks[3], (DEPTH, D_MODEL, 2 * D_FF), D_MODEL),
        "ffn1_w_down": w(ks[4], (DEPTH, D_FF, D_MODEL), D_FF),
        "mix_norm": g(ks[5], (DEPTH, D_MODEL)),
        "w_o": w(ks[6], (DEPTH, MIX_W, D_MODEL), MIX_W),
        "mem_norm": g(ks[7], (DEPTH, D_MODEL)),
        "w_mem_kv": w(ks[8], (DEPTH, D_MODEL, 2 * MEM_W), D_MODEL),
        "mem_q_norm": g(ks[9], (DEPTH, MEM_HD)),
        "mem_k_norm": g(ks[10], (DEPTH, MEM_HD)),
        "ffn2_norm": g(ks[11], (DEPTH, D_MODEL)),
        "ffn2_w_gu": w(ks[12], (DEPTH, D_MODEL, 2 * D_FF), D_MODEL),
        "ffn2_w_down": w(ks[13], (DEPTH, D_FF, D_MODEL), D_FF),
        "a_w_in": w(ks[14], (N_A, D_MODEL, A_IN), D_MODEL),
        "a_q_a_norm": g(ks[15], (N_A, A_Q_LORA)),
        "a_kv_a_norm": g(ks[16], (N_A, A_KV_LORA)),
        "a_w_q_b": w(ks[17], (N_A, A_Q_LORA, A_HEADS * (A_NOPE + A_ROPE)), A_Q_LORA),
        "a_w_kv_b": w(ks[18], (N_A, A_KV_LORA, A_HEADS * (A_NOPE + A_V)), A_KV_LORA),
        "a_q_nope_norm": g(ks[19], (N_A, A_NOPE)),
        "a_q_pe_norm": g(ks[20], (N_A, A_ROPE)),
        "a_k_nope_norm": g(ks[21], (N_A, A_NOPE)),
        "a_k_pe_norm": g(ks[22], (N_A, A_ROPE)),
        "b_w_in": w(ks[23], (N_B, D_MODEL, B_IN), D_MODEL),
        "b_q_norm": g(ks[24], (N_B, B_HD)),
        "b_k_norm": g(ks[25], (N_B, B_HD)),
    }


def reference(x, mem, ffn1_norm, ffn1_w_gu, ffn1_w_down, mix_norm, w_o,
              mem_norm, w_mem_kv, mem_q_norm, mem_k_norm,
              ffn2_norm, ffn2_w_gu, ffn2_w_down,
              a_w_in, a_q_a_norm, a_kv_a_norm, a_w_q_b, a_w_kv_b,
              a_q_nope_norm, a_q_pe_norm, a_k_nope_norm, a_k_pe_norm,
              b_w_in, b_q_norm, b_k_norm):
    b, s, _ = x.shape
    m = mem.shape[1]
    cos_a, sin_a = axial_rope_tables(s, A_ROPE)
    cos_b, sin_b = axial_rope_tables(s, B_HD)
    for i in range(DEPTH):
        x = x + 0.5 * swiglu(rms_norm(x, ffn1_norm[i]), ffn1_w_gu[i], ffn1_w_down[i])
        u = rms_norm(x, mix_norm[i])
        j = i // N_MIXERS
        if i % N_MIXERS == 0:
            tok, q_mem = mla_mixer(u, a_w_in[j], a_q_a_norm[j], a_kv_a_norm[j], a_w_q_b[j], a_w_kv_b[j],
                                   a_q_nope_norm[j], a_q_pe_norm[j], a_k_nope_norm[j], a_k_pe_norm[j],
                                   cos_a, sin_a)
        else:
            tok, q_mem = gqa_mixer(u, b_w_in[j], b_q_norm[j], b_k_norm[j], cos_b, sin_b)
        mem_kv = (rms_norm(mem, mem_norm[i]) @ w_mem_kv[i]).reshape(b, m, 2, MEM_HEADS, MEM_HD)
        mk = rms_norm(mem_kv[:, :, 0], mem_k_norm[i])
        mv = mem_kv[:, :, 1]
        mq = rms_norm(q_mem.reshape(b, s, MEM_HEADS, MEM_HD), mem_q_norm[i])
        mo = memory_attention(mq, mk, mv)
        x = x + jnp.concatenate([tok, mo], axis=-1) @ w_o[i]
        x = x + 0.5 * swiglu(rms_norm(x, ffn2_norm[i]), ffn2_w_gu[i], ffn2_w_down[i])
    return x
```

```python
import numpy as np
import ml_dtypes
from contextlib import ExitStack
import concourse.bass as bass
import concourse.mybir as mybir
from concourse.bass_utils import run_bass_kernel_spmd

F32 = mybir.dt.float32
BF16 = mybir.dt.bfloat16
AF = mybir.ActivationFunctionType
ALU = mybir.AluOpType
EPS = 1e-6
NCORE = 8
G4 = [[0, 1, 2, 3], [4, 5, 6, 7]]
G2 = [[0, 4], [1, 5], [2, 6], [3, 7]]
QOS = "P1"
PMAX = 512 * 1024


def full_cfg():
    return dict(D=4096, FF=6144, S=8192, DEPTH=4, H=24, KVH=6, QL=1024, KVL=512,
                MT=256, MH=4, MHD=256, TB=512, GRID_W=64, THETA=10000.0)


def weight_specs(cfg, kind):
    D, FF, H, KVH, QL, KVL, MH, MHD = (cfg[k] for k in ("D", "FF", "H", "KVH", "QL", "KVL", "MH", "MHD"))
    MW = MH * MHD
    sp = {}
    sp["gu1"] = (D, 2 * FF, 256); sp["dn1"] = (FF, D, 128)
    if kind == 0:
        sp["cq"] = (D, QL, 128); sp["ckv"] = (D, KVL, 128); sp["kpe"] = (D, 64, 64)
        sp["qm"] = (D, MW, 128)
        HG = 4 if H % 4 == 0 else 2
        sp["qn"] = (QL, H * 128, HG * 128); sp["qp"] = (QL, H * 64, 128)
        sp["kn"] = (KVL, H * 128, HG * 128); sp["vv"] = (KVL, H * 128, HG * 128)
    else:
        sp["wq"] = (D, H * 128, 128); sp["wk"] = (D, KVH * 128, 128); sp["wv"] = (D, KVH * 128, 256)
        sp["qm"] = (D, MW, 128)
    sp["mk"] = (D, MW, 128); sp["mv"] = (D, MW, 256)
    sp["wo"] = (H * 128 + MW, D, 128)
    sp["gu2"] = (D, 2 * FF, 256); sp["dn2"] = (FF, D, 128)
    return sp


def piece_size(E):
    per = E // NCORE
    assert per * NCORE == E
    k = -(-per // PMAX)
    while per % k or (per // k) % 512:
        k += 1
    return per // k, k


def block_weight(W, CW):
    K, N = W.shape
    return np.ascontiguousarray(W.reshape(K // 128, 128, N // CW, CW).transpose(2, 1, 0, 3))


def shard_weight(Wb):
    flat = Wb.reshape(-1)
    P, k = piece_size(flat.size)
    u = flat.reshape(k, NCORE, P)
    return [np.ascontiguousarray(u[:, c, :]).reshape(-1, 512) for c in range(NCORE)]


class Sem:
    def __init__(self, h):
        self.h = h
        self.n = 0


class Buf:
    def __init__(self, kb, name, dma=False):
        self.name = name
        self.w = {}
        self.r = {}
        self.dsem = kb.get_dsem(name) if dma else None


class KB:
    def __init__(self, nc, es):
        self.nc = nc
        self.es = es
        self.nsem = 0
        self.dsems = set()
        self.engs = {"pe": nc.tensor, "act": nc.scalar, "dve": nc.vector, "sp": nc.sync, "pool": nc.gpsimd}
        self.esem = {k: self.new_sem("e_" + k) for k in ("pe", "act", "dve")}
        self.known = {k: {} for k in self.engs}
        self.free_dsems = []
        self.phase_dsems = []
        self.in_phase = False

    def new_sem(self, name):
        self.nsem += 1
        return Sem(self.es.enter_context(self.nc.semaphore(f"{name}_{self.nsem}")))

    def get_dsem(self, name):
        if self.free_dsems:
            sm = self.free_dsems.pop()
        else:
            sm = self.new_sem("d_" + name)
            self.dsems.add(sm)
        if self.in_phase:
            self.phase_dsems.append(sm)
        return sm

    def dma_buf(self, name):
        return Buf(self, name, dma=True)

    def end_phase(self):
        self.barrier()
        self.free_dsems.extend(self.phase_dsems)
        self.phase_dsems = []

    def wait(self, eng, sem, val):
        if val <= 0:
            return
        if sem in self.dsems:
            val = sem.n
        kn = self.known[eng]
        if kn.get(sem, 0) >= val:
            return
        self.engs[eng].wait_ge(sem.h, val)
        kn[sem] = val

    def _deps(self, eng, reads, writes, accs):
        for b in list(reads) + list(accs):
            for s, v in b.w.items():
                self.wait(eng, s, v)
        for b in writes:
            for s, v in list(b.w.items()) + list(b.r.items()):
                self.wait(eng, s, v)

    def _commit(self, ev, reads, writes, accs):
        s, v = ev
        for b in reads:
            b.r[s] = v
        for b in writes:
            b.w = {s: v}
            b.r = {}
        for b in accs:
            b.w[s] = v

    def op(self, eng, fn, reads=(), writes=(), accs=()):
        self._deps(eng, reads, writes, accs)
        ins = fn()
        s = self.esem[eng]
        ins.then_inc(s.h, 1)
        s.n += 1
        self._commit((s, s.n), reads, writes, accs)

    def dma(self, eng, out, in_, sb, reads=(), writes=(), accs=(), **kw):
        self._deps(eng, reads, writes, accs)
        ins = self.engs[eng].dma_start(out=out, in_=in_, **kw)
        s = sb.dsem
        ins.then_inc(s.h, 16)
        s.n += 16
        self._commit((s, s.n), reads, writes, accs)

    def barrier(self, engs=("pe", "act", "dve", "sp")):
        for e in engs:
            for s in list(self.esem.values()) + list(self.dsems):
                self.wait(e, s, s.n)


def dap(t, off, dims):
    return bass.AP(t, off, [list(d) for d in dims])


class Prog:
    def __init__(self, cfg):
        self.cfg = cfg
        c = cfg
        self.D, self.FF, self.S, self.TB = c["D"], c["FF"], c["S"], c["TB"]
        self.NT = self.S // 4
        self.NB = self.NT // self.TB
        self.KD = self.D // 128
        self.H, self.KVH = c["H"], c["KVH"]
        self.MH, self.MHD, self.MT = c["MH"], c["MHD"], c["MT"]
        self.MW = self.MH * self.MHD
        self.OC = self.H + self.MW // 128
        self.kinds = [i % 2 for i in range(c["DEPTH"])]
        self.gcol = {}
        n = 0
        for i, kd in enumerate(self.kinds):
            names = [("ffn1", self.KD), ("mix", self.KD), ("memn", self.KD), ("memq", self.MHD // 128),
                     ("memk", self.MHD // 128), ("ffn2", self.KD)]
            if kd == 0:
                names += [("qa", c["QL"] // 128), ("kva", c["KVL"] // 128), ("qnope", 1), ("qpe", 1), ("knope", 1), ("kpe", 1)]
            else:
                names += [("bq", 1), ("bk", 1)]
            for nm, w in names:
                self.gcol[(i, nm)] = n
                n += w
        self.NG = n

    def host_inputs(self, inp):
        c = self.cfg
        D, FF, H, KVH, QL, KVL, MW = self.D, self.FF, self.H, self.KVH, c["QL"], c["KVL"], self.MW
        NT, KD = self.NT, self.KD
        maps = [dict() for _ in range(NCORE)]
        x = np.asarray(inp["x"]); mem = np.asarray(inp["mem"])
        for cc in range(NCORE):
            b, q = cc // 4, cc % 4
            maps[cc]["x"] = np.ascontiguousarray(x[b, q * NT:(q + 1) * NT, :].T).reshape(KD, 128, NT)
            maps[cc]["mem"] = np.ascontiguousarray(mem[b].T).reshape(KD, 128, self.MT)
        gv = np.zeros((128, self.NG), np.float32)

        def put(i, nm, vec):
            col = self.gcol[(i, nm)]
            vec = np.asarray(vec, np.float32)
            if vec.size == 64:
                vec = np.concatenate([vec, vec])
            w = vec.size // 128
            gv[:, col:col + w] = vec.reshape(w, 128).T
        for i, kd in enumerate(self.kinds):
            j = i // 2
            put(i, "ffn1", inp["ffn1_norm"][i]); put(i, "mix", inp["mix_norm"][i]); put(i, "memn", inp["mem_norm"][i])
            put(i, "memq", inp["mem_q_norm"][i]); put(i, "memk", inp["mem_k_norm"][i]); put(i, "ffn2", inp["ffn2_norm"][i])
            if kd == 0:
                put(i, "qa", inp["a_q_a_norm"][j]); put(i, "kva", inp["a_kv_a_norm"][j])
                put(i, "qnope", inp["a_q_nope_norm"][j]); put(i, "qpe", inp["a_q_pe_norm"][j])
                put(i, "knope", inp["a_k_nope_norm"][j]); put(i, "kpe", inp["a_k_pe_norm"][j])
            else:
                put(i, "bq", inp["b_q_norm"][j]); put(i, "bk", inp["b_k_norm"][j])
        cb = np.zeros((128, 256), np.float32)
        cb[:, :128] = 1.0
        cb[:64, 128:192] = 1.0
        cb[64:, 192:256] = 1.0
        perm = np.zeros((128, 128), np.float32)
        for p in range(128):
            perm[p ^ 1, p] = 1.0
        GW = c["GRID_W"]

        def tables(tok0, rot_dim, nrep):
            t = np.arange(tok0, tok0 + NT)
            row = (t // GW).astype(np.float32); col = (t % GW).astype(np.float32)
            ad = rot_dim // 2
            inv = (np.float32(c["THETA"]) ** (-np.arange(0, ad, 2, dtype=np.float32) / np.float32(ad))).astype(np.float32)
            ang = np.concatenate([row[:, None] * inv, col[:, None] * inv], -1).astype(np.float32)
            cos = np.cos(ang).astype(np.float32); sin = np.sin(ang).astype(np.float32)
            C = np.repeat(cos, 2, axis=1)
            Sg = np.repeat(sin, 2, axis=1)
            Sg[:, 0::2] *= -1.0
            C = np.tile(C, (1, nrep)); Sg = np.tile(Sg, (1, nrep))
            return np.ascontiguousarray(np.stack([C.T, Sg.T], axis=1))
        for cc in range(NCORE):
            q = cc % 4
            maps[cc]["gv"] = gv
            maps[cc]["cbf"] = cb.astype(ml_dtypes.bfloat16)
            maps[cc]["perm"] = perm
            maps[cc]["csa"] = tables(q * NT, 64, 2)
            maps[cc]["csb"] = tables(q * NT, 128, 1)
        for i, kd in enumerate(self.kinds):
            j = i // 2
            W = {}
            for t, nm in ((1, "ffn1"), (2, "ffn2")):
                gu = np.asarray(inp[f"{nm}_w_gu"][i])
                g = gu[:, :FF].reshape(D, FF // 128, 128); u = gu[:, FF:].reshape(D, FF // 128, 128)
                W[f"gu{t}"] = np.concatenate([g, u], axis=2).reshape(D, 2 * FF)
                W[f"dn{t}"] = np.asarray(inp[f"{nm}_w_down"][i])
            if kd == 0:
                win = np.asarray(inp["a_w_in"][j])
                W["cq"] = win[:, :QL]; W["ckv"] = win[:, QL:QL + KVL]; W["kpe"] = win[:, QL + KVL:QL + KVL + 64]
                W["qm"] = win[:, QL + KVL + 64:]
                wqb = np.asarray(inp["a_w_q_b"][j]).reshape(QL, H, 192)
                W["qn"] = wqb[:, :, :128].reshape(QL, H * 128); W["qp"] = wqb[:, :, 128:].reshape(QL, H * 64)
                wkv = np.asarray(inp["a_w_kv_b"][j]).reshape(KVL, H, 256)
                W["kn"] = wkv[:, :, :128].reshape(KVL, H * 128); W["vv"] = wkv[:, :, 128:].reshape(KVL, H * 128)
            else:
                win = np.asarray(inp["b_w_in"][j])
                qw, kw = H * 128, KVH * 128
                W["wq"] = win[:, :qw]; W["wk"] = win[:, qw:qw + kw]; W["wv"] = win[:, qw + kw:qw + 2 * kw]
                W["qm"] = win[:, qw + 2 * kw:]
            mkv = np.asarray(inp["w_mem_kv"][i])
            W["mk"] = mkv[:, :MW]; W["mv"] = mkv[:, MW:]
            W["wo"] = np.asarray(inp["w_o"][i])
            for nm, (K, N, CW) in weight_specs(self.cfg, kd).items():
                assert W[nm].shape == (K, N), (nm, W[nm].shape, K, N)
                sh = shard_weight(block_weight(np.ascontiguousarray(W[nm], dtype=np.float32), CW))
                for cc in range(NCORE):
                    maps[cc][f"w{i}_{nm}"] = sh[cc]
        return maps

    def build(self):
        c = self.cfg
        nc = bass.Bass("TRN2", target_bir_lowering=False)
        self.nc = nc
        NT, KD, H, KVH = self.NT, self.KD, self.H, self.KVH
        with ExitStack() as es:
            kb = KB(nc, es)
            self.kb = kb
            dt = lambda name, shape, dty, kind: nc.dram_tensor(name, list(shape), dty, kind=kind)
            self.x_in = dt("x", (KD, 128, NT), F32, "ExternalInput")
            self.mem_in = dt("mem", (KD, 128, self.MT), F32, "ExternalInput")
            self.gv_in = dt("gv", (128, self.NG), F32, "ExternalInput")
            self.cbf_in = dt("cbf", (128, 256), BF16, "ExternalInput")
            self.perm_in = dt("perm", (128, 128), F32, "ExternalInput")
            self.csa_in = dt("csa", (128, 2, NT), F32, "ExternalInput")
            self.csb_in = dt("csb", (128, 2, NT), F32, "ExternalInput")
            self.xr = dt("out", (KD, 128, NT), F32, "ExternalOutput")
            self.wext, self.wg, self.wspec = {}, {}, {}
            for i, kd in enumerate(self.kinds):
                for nm, (K, N, CW) in weight_specs(c, kd).items():
                    E = K * N
                    P, k = piece_size(E)
                    key = (i, nm)
                    self.wext[key] = dt(f"w{i}_{nm}", (k * P // 512, 512), F32, "ExternalInput")
                    self.wg[key] = dt(f"g{i}_{nm}", (E // 512, 512), BF16, "Internal")
                    self.wspec[key] = (K, N, CW, P, k)
            self.castb = [dt(f"castb{t}", (PMAX // 512, 512), BF16, "Internal") for t in range(2)]
            self.pairb = dt("pairb", (2 * PMAX // 512, 512), BF16, "Internal")
            self.OT = dt("OT", (self.OC, 128, NT), BF16, "Internal")
            self.QS = dt("QS", (2 * H, 128, NT), BF16, "Internal")
            self.QM = dt("QM", (self.MW // 128, 128, NT), BF16, "Internal")
            self.GL = dt("GL", (8, 128, NT), BF16, "Internal")
            self.GG = dt("GG", (8, 4 * 128, NT), BF16, "Internal")
            self.VL = dt("VL", (NT, KVH * 128), BF16, "Internal")
            self.VG = dt("VG", (4 * NT, KVH * 128), BF16, "Internal")
            self.KN = dt("KN", (H, 128, 4 * NT), BF16, "Internal")
            self.VA = dt("VA", (H, 128, 4 * NT // 128, 128), BF16, "Internal")
            self.MK = dt("MK", (self.MW // 128, 128, self.MT), BF16, "Internal")
            self.MV = dt("MV", (self.MT, self.MW), BF16, "Internal")
            sb = lambda name, shape, dty: es.enter_context(nc.sbuf_tensor(name, list(shape), dty))
            self.NSLOT = c.get("NSLOT", 4)
            self.WS = 8192
            self.ring = sb("ring", (128, self.NSLOT, self.WS), BF16)
            self.ring_b = [kb.dma_buf(f"ring{t}") for t in range(self.NSLOT)]
            self.ring_i = 0
            self.lring = None
            self.gv = sb("gvs", (128, self.NG), F32)
            self.cbf = sb("cbfs", (128, 256), BF16)
            self.perm = sb("perms", (128, 128), F32)
            self.epsc = sb("epsc", (128, 1), F32)
            self.onesf = sb("onesf", (128, 128), F32)
            cb = kb.dma_buf("consts")
            self.cbuf = cb
            kb.dma("sp", self.gv[:], self.gv_in.ap(), cb, writes=[cb])
            kb.dma("sp", self.cbf[:], self.cbf_in.ap(), cb, accs=[cb])
            kb.dma("sp", self.perm[:], self.perm_in.ap(), cb, accs=[cb])
            kb.op("dve", lambda: nc.vector.memset(self.epsc[:], EPS), accs=[cb])
            kb.op("dve", lambda: nc.vector.memset(self.onesf[:], 1.0), accs=[cb])
            self.ps = [es.enter_context(nc.psum_tensor(f"ps{t}", [128, 512], F32)) for t in range(8)]
            self.psb = [Buf(kb, f"ps{t}") for t in range(8)]
            self.cc = kb.new_sem("cc")
            self.castsem = [kb.new_sem("cast0"), kb.new_sem("cast1")]
            self.ph = kb.new_sem("ph")
            self.wready = {}
            self.wunit = {}
            self.unit_i = 0
            self.uid = 0
            kb.barrier()
            kb.in_phase = True

            nl = len(self.kinds)
            self.pool_weights(0)
            import os
            stop = int(os.environ.get("KSTOP", "1000"))
            steps = []
            for i, kd in enumerate(self.kinds):
                src = self.x_in if i == 0 else self.xr
                steps.append(lambda i=i, src=src: self.ffn(i, 1, src))
                steps.append(lambda i=i, kd=kd: self.mixer_in(i, kd))

                def gath(i=i, kd=kd):
                    self.phase_sync_pool()
                    self.pool_kv_gather(i, kd)
                    if i + 1 < nl:
                        self.pool_weights(i + 1)
                steps.append(gath)
                steps.append(lambda i=i: self.mem_kv(i))
                if kd == 0:
                    steps.append(lambda i=i: self.mla_kv_expand(i))
                steps.append(lambda i=i, kd=kd: self.attention(i, kd))
                steps.append(lambda i=i: self.out_proj(i))
                steps.append(lambda i=i: self.ffn(i, 2, self.xr))
            for st in steps[:stop]:
                st()
            kb.barrier()
        return nc

    def tag(self, s):
        self.uid += 1
        return f"{s}{self.uid}"

    def pool_weights(self, i):
        nc = self.nc
        g = nc.gpsimd
        kd = self.kinds[i]
        units = []
        for nm in weight_specs(self.cfg, kd):
            key = (i, nm)
            for u in range(self.wspec[key][4]):
                units.append((key, u))

        def cast(ix):
            key, u = units[ix]
            P = self.wspec[key][3]
            R = P // 512
            t = (self.unit_i + ix) % 2
            cs = self.castsem[t]
            src = dap(self.wext[key], u * P, [[512, R], [1, 512]])
            cbv = dap(self.castb[t], 0, [[512, R], [1, 512]])
            g.dma_start(out=cbv, in_=src).then_inc(cs.h, 16)
            cs.n += 16
            return cs.n
        cast_tgt = {0: cast(0)}
        for ix, (key, u) in enumerate(units):
            if True:
                K, N, CW, P, k = self.wspec[key]
                R = P // 512
                t = (self.unit_i + ix) % 2
                cs = self.castsem[t]
                cbv = dap(self.castb[t], 0, [[512, R], [1, 512]])
                g.wait_ge(cs.h, cast_tgt[ix])
                pv = dap(self.pairb, 0, [[512, 2 * R], [1, 512]])
                g.collective_compute("AllGather", ALU.bypass, replica_groups=G2, ins=[cbv], outs=[pv], dma_qos=QOS).then_inc(self.cc.h, 1)
                self.cc.n += 1
                if ix + 1 < len(units):
                    cast_tgt[ix + 1] = cast(ix + 1)
                g.wait_ge(self.cc.h, self.cc.n)
                for hh in range(2):
                    iv = dap(self.pairb, hh * P, [[512, R], [1, 512]])
                    ov = dap(self.wg[key], u * 8 * P + hh * 4 * P, [[512, 4 * R], [1, 512]])
                    g.collective_compute("AllGather", ALU.bypass, replica_groups=G4, ins=[iv], outs=[ov], dma_qos=QOS).then_inc(self.cc.h, 1)
                    self.cc.n += 1
                g.wait_ge(self.cc.h, self.cc.n)
                self.wunit.setdefault(key, []).append(self.cc.n)
                self.wready[key] = self.cc.n
        self.unit_i += len(units)

    def phase_sync_pool(self):
        kb = self.kb
        kb.barrier()
        self.nc.sync.nop().then_inc(self.ph.h, 1)
        self.ph.n += 1
        self.nc.gpsimd.wait_ge(self.ph.h, self.ph.n)

    def pool_kv_gather(self, i, kd):
        g = self.nc.gpsimd
        NT = self.NT
        assert 128 * NT <= PMAX
        nch = 5 if kd == 0 else self.KVH
        for ch in range(nch):
            rows = 64 if (kd == 0 and ch == 4) else 128
            iv = dap(self.GL, ch * 128 * NT, [[NT, rows], [1, NT]])
            ov = dap(self.GG, ch * 512 * NT, [[NT, 4 * rows], [1, NT]])
            g.collective_compute("AllGather", ALU.bypass, replica_groups=G4, ins=[iv], outs=[ov]).then_inc(self.cc.h, 1)
            self.cc.n += 1
            g.wait_ge(self.cc.h, self.cc.n)
        if kd == 1:
            W = self.KVH * 128
            rstep = 128
            while rstep * 2 * W <= PMAX and NT % (rstep * 2) == 0:
                rstep *= 2
            self.v_rstep = rstep
            for j in range(NT // rstep):
                iv = dap(self.VL, j * rstep * W, [[W, rstep], [1, W]])
                ov = dap(self.VG, j * 4 * rstep * W, [[W, 4 * rstep], [1, W]])
                g.collective_compute("AllGather", ALU.bypass, replica_groups=G4, ins=[iv], outs=[ov]).then_inc(self.cc.h, 1)
                self.cc.n += 1
                g.wait_ge(self.cc.h, self.cc.n)
        self.kv_ready = self.cc.n

    def wtile_load(self, key, t):
        kb = self.kb
        K, N, CW, P, k = self.wspec[key]
        n = (K // 128) * CW
        if self.lring is not None and n <= self.lring["ws"]:
            L = self.lring
            slot = L["i"] % L["n"]
            L["i"] += 1
            b = L["b"][slot]
            u = ((t + 1) * 128 * n - 1) // (8 * P)
            kb.wait("sp", self.cc, self.wunit[key][u])
            src = dap(self.wg[key], t * 128 * n, [[n, 128], [1, n]])
            kb.dma("sp", L["t"][:, slot, 0:n], src, b, writes=[b])
            return ("L", slot)
        assert n <= self.WS
        slot = self.ring_i % self.NSLOT
        self.ring_i += 1
        b = self.ring_b[slot]
        u = ((t + 1) * 128 * n - 1) // (8 * P)
        kb.wait("sp", self.cc, self.wunit[key][u])
        src = dap(self.wg[key], t * 128 * n, [[n, 128], [1, n]])
        kb.dma("sp", self.ring[:, slot, 0:n], src, b, writes=[b])
        return slot

    def wview(self, slot, key, kt, c0, cw):
        CW = self.wspec[key][2]
        if isinstance(slot, tuple):
            return self.lring["t"][:, slot[1], kt * CW + c0: kt * CW + c0 + cw]
        return self.ring[:, slot, kt * CW + c0: kt * CW + c0 + cw]

    def rbuf(self, slot):
        if isinstance(slot, tuple):
            return self.lring["b"][slot[1]]
        return self.ring_b[slot]

    def make_lring(self, es, nslots, ws):
        nc, kb = self.nc, self.kb
        tg = self.tag("lr")
        self.lring = dict(t=es.enter_context(nc.sbuf_tensor(f"{tg}_t", [128, nslots, ws], BF16)),
                          b=[kb.dma_buf(f"{tg}{t}") for t in range(nslots)], i=0, n=nslots, ws=ws)

    def stream_tiles(self, key, ntiles, body):
        K, N, CW, P, k = self.wspec[key]
        n = (K // 128) * CW
        local = self.lring is not None and n <= self.lring["ws"]
        ahead = (self.lring["n"] - 1) if local else (self.NSLOT - 1)
        slots = {}
        for t in range(min(ahead, ntiles)):
            slots[t] = self.wtile_load(key, t)
        for t in range(ntiles):
            body(t, slots[t], self.rbuf(slots[t]))
            if t + ahead < ntiles:
                slots[t + ahead] = self.wtile_load(key, t + ahead)

    def gcolap(self, i, nm, j=0, rows=128):
        col = self.gcol[(i, nm)] + j
        return self.gv[0:rows, col:col + 1]

    def small_arena(self, es, w):
        nc, kb = self.nc, self.kb
        tg = self.tag("a")
        sb = lambda name, shape, dty: es.enter_context(nc.sbuf_tensor(f"{tg}_{name}", list(shape), dty))
        A = {"w": w}
        A["xc"] = [sb(f"xc{t}", (128, w), F32) for t in range(4)]
        A["xcb"] = [kb.dma_buf(f"{tg}xc{t}") for t in range(4)]
        A["sq"] = [sb(f"sq{t}", (128, w), BF16) for t in range(2)]
        A["sqb"] = [Buf(kb, f"{tg}sq{t}") for t in range(2)]
        A["t32"] = sb("t32", (128, w), F32)
        A["t32b"] = Buf(kb, f"{tg}t32")
        A["rstd"] = sb("rstd", (128, w), F32)
        A["rstdb"] = Buf(kb, f"{tg}rstd")
        return A

    def nl_arena(self, es, w, gsmax=8, rope=False):
        nc, kb = self.nc, self.kb
        tg = self.tag("n")
        sb = lambda name, shape, dty: es.enter_context(nc.sbuf_tensor(f"{tg}_{name}", list(shape), dty))
        N = {"w": w, "gsmax": gsmax}
        N["g32"] = sb("g32", (128, 8, w), F32)
        N["g32b"] = [Buf(kb, f"{tg}g{t}") for t in range(8)]
        N["sqg"] = sb("sqg", (128, 8, w), BF16)
        N["sqgb"] = [Buf(kb, f"{tg}q{t}") for t in range(8)]
        N["tmp"] = [sb(f"tmp{t}", (128, w), F32) for t in range(4)]
        N["tmpb"] = [Buf(kb, f"{tg}tmp{t}") for t in range(4)]
        N["rs"] = [sb(f"rs{t}", (128, w), F32) for t in range(4)]
        N["rsb"] = [Buf(kb, f"{tg}rs{t}") for t in range(4)]
        nst = 8 if gsmax == 1 else 3
        N["st"] = [sb(f"st{t}", (128, min(gsmax, 4), w), BF16) for t in range(nst)]
        N["stb"] = [kb.dma_buf(f"{tg}st{t}") for t in range(nst)]
        N["gi"] = 0
        if rope:
            for nm in ("n32", "a32", "b32"):
                N[nm] = [sb(f"{nm}{t}", (128, w), F32) for t in range(3)]
                N[nm + "b"] = [Buf(kb, f"{tg}{nm}{t}") for t in range(3)]
        return N

    def norm_from_dram(self, src_fn, KT, w, i, gname, out_sb, out_b, A):
        nc, kb = self.nc, self.kb
        xc, xcb, sq, sqb = A["xc"], A["xcb"], A["sq"], A["sqb"]
        ones = self.cbf[:, 0:128]
        ps, pb = self.ps[6], self.psb[6]
        for kt in range(KT):
            s = kt % 4
            q = kt % 2
            kb.dma("sp", xc[s][:, 0:w], src_fn(kt), xcb[s], writes=[xcb[s]])
            kb.op("dve", lambda: nc.vector.tensor_tensor(out=sq[q][:, 0:w], in0=xc[s][:, 0:w], in1=xc[s][:, 0:w], op=ALU.mult),
                  reads=[xcb[s]], writes=[sqb[q]])
            kb.op("pe", lambda: nc.tensor.matmul(ps[:, 0:w], ones, sq[q][:, 0:w], start=(kt == 0), stop=(kt == KT - 1)),
                  reads=[sqb[q], self.cbuf], writes=[pb] if kt == 0 else [], accs=[] if kt == 0 else [pb])
        kb.op("act", lambda: nc.scalar.activation(out=A["t32"][:, 0:w], in_=ps[:, 0:w], func=AF.Ln, scale=1.0 / (KT * 128), bias=self.epsc[:, 0:1]),
              reads=[pb, self.cbuf], writes=[A["t32b"]])
        kb.op("act", lambda: nc.scalar.activation(out=A["rstd"][:, 0:w], in_=A["t32"][:, 0:w], func=AF.Exp, scale=-0.5),
              reads=[A["t32b"]], writes=[A["rstdb"]])
        for kt in range(KT):
            s = kt % 4
            kb.dma("sp", xc[s][:, 0:w], src_fn(kt), xcb[s], writes=[xcb[s]])
            kb.op("dve", lambda: nc.vector.scalar_tensor_tensor(out=out_sb[:, kt, 0:w], in0=xc[s][:, 0:w], scalar=self.gcolap(i, gname, kt),
                                                              in1=A["rstd"][:, 0:w], op0=ALU.mult, op1=ALU.mult),
                  reads=[xcb[s], A["rstdb"], self.cbuf], writes=[out_b] if kt == 0 else [], accs=[] if kt == 0 else [out_b])

    def norm_linear(self, key, rhs_fn, rhs_bufs, w, gs, nfeat, i, gname, N, consumer, rows=128, ones=None,
                    rope=None, t0=0, banks=(0, 1, 2, 3), aux=(4, 5, 6, 7), fixed_out=None, gain_per_chunk=True):
        nc, kb = self.nc, self.kb
        K, Nn, CW, P, k = self.wspec[key]
        KT = K // 128
        ntiles = Nn // CW
        if ones is None:
            ones = self.cbf[:, 0:128]
        cnt = [0]
        auxc = [0]
        pend = []
        finq = []
        lag = 2 if gs == 1 else (1 if gs == 2 else 0)

        def finish(grp, slots):
            par = N["gi"] % 4
            par3 = N["gi"] % 3
            N["gi"] += 1
            bi = aux[auxc[0] % len(aux)]
            auxc[0] += 1
            ps, pb = self.ps[bi], self.psb[bi]

            def mm():
                ins = None
                for jj, sl in enumerate(slots):
                    ins = nc.tensor.matmul(ps[0:rows, 0:w], ones[0:rows, 0:rows], N["sqg"][0:rows, sl, 0:w],
                                           start=(jj == 0), stop=(jj == len(slots) - 1))
                return ins
            kb.op("pe", mm, reads=[N["sqgb"][sl] for sl in slots] + [self.cbuf], writes=[pb])
            tmp, tmpb, rs, rsb = N["tmp"][par], N["tmpb"][par], N["rs"][par], N["rsb"][par]
            kb.op("act", lambda: nc.scalar.activation(out=tmp[0:rows, 0:w], in_=ps[0:rows, 0:w], func=AF.Ln, scale=1.0 / nfeat, bias=self.epsc[0:rows, 0:1]),
                  reads=[pb, self.cbuf], writes=[tmpb])
            kb.op("act", lambda: nc.scalar.activation(out=rs[0:rows, 0:w], in_=tmp[0:rows, 0:w], func=AF.Exp, scale=-0.5),
                  reads=[tmpb], writes=[rsb])
            if fixed_out is not None:
                ot, otb = fixed_out
                ofn = lambda jj: ot[0:rows, grp * gs + jj, 0:w]
            else:
                si = N["gi"] % len(N["st"])
                ot, otb = N["st"][si], N["stb"][si]
                ofn = lambda jj: ot[0:rows, jj, 0:w]
            for jj, sl in enumerate(slots):
                gap = self.gcolap(i, gname, jj if gain_per_chunk else 0, rows)
                g32 = N["g32"][0:rows, sl, 0:w]
                first = (jj == 0) and (fixed_out is None or grp == 0)
                wr = dict(writes=[otb]) if first else dict(accs=[otb])
                if rope is None:
                    kb.op("dve", lambda: nc.vector.scalar_tensor_tensor(out=ofn(jj), in0=g32, scalar=gap, in1=rs[0:rows, 0:w],
                                                                      op0=ALU.mult, op1=ALU.mult),
                          reads=[N["g32b"][sl], rsb, self.cbuf], **wr)
                else:
                    cs = rope
                    n32, n32b = N["n32"][par3], N["n32b"][par3]
                    a32, a32b = N["a32"][par3], N["a32b"][par3]
                    b32, b32b = N["b32"][par3], N["b32b"][par3]
                    kb.op("dve", lambda: nc.vector.scalar_tensor_tensor(out=n32[0:rows, 0:w], in0=g32, scalar=gap, in1=rs[0:rows, 0:w],
                                                                      op0=ALU.mult, op1=ALU.mult),
                          reads=[N["g32b"][sl], rsb, self.cbuf], writes=[n32b])
                    b2 = aux[auxc[0] % len(aux)]
                    auxc[0] += 1
                    ps2, pb2 = self.ps[b2], self.psb[b2]
                    kb.op("pe", lambda: nc.tensor.matmul(ps2[0:rows, 0:w], self.perm[0:rows, 0:rows], n32[0:rows, 0:w], start=True, stop=True),
                          reads=[n32b, self.cbuf], writes=[pb2])
                    kb.op("dve", lambda: nc.vector.tensor_tensor(out=a32[0:rows, 0:w], in0=n32[0:rows, 0:w], in1=cs[0:rows, 0, t0:t0 + w], op=ALU.mult),
                          reads=[n32b, self.csbuf], writes=[a32b])
                    kb.op("dve", lambda: nc.vector.tensor_tensor(out=b32[0:rows, 0:w], in0=ps2[0:rows, 0:w], in1=cs[0:rows, 1, t0:t0 + w], op=ALU.mult),
                          reads=[pb2, self.csbuf], writes=[b32b])
                    kb.op("dve", lambda: nc.vector.tensor_tensor(out=ofn(jj), in0=a32[0:rows, 0:w], in1=b32[0:rows, 0:w], op=ALU.add),
                          reads=[a32b, b32b], **wr)
            if consumer is not None:
                pend.append((grp, ofn, otb))
                while len(pend) > 1:
                    consumer(*pend.pop(0))

        def body(t, slot, rb):
            for c0 in range(0, CW, rows):
                ch = cnt[0]
                cnt[0] += 1
                bi = banks[ch % len(banks)]
                ps, pb = self.ps[bi], self.psb[bi]

                def mm():
                    ins = None
                    for kt in range(KT):
                        ins = nc.tensor.matmul(ps[0:rows, 0:w], self.wview(slot, key, kt, c0, rows), rhs_fn(kt),
                                               start=(kt == 0), stop=(kt == KT - 1))
                    return ins
                kb.op("pe", mm, reads=[rb] + list(rhs_bufs), writes=[pb])
                sl = ch % 8
                kb.op("act", lambda: nc.scalar.copy(out=N["g32"][0:rows, sl, 0:w], in_=ps[0:rows, 0:w]),
                      reads=[pb], writes=[N["g32b"][sl]])
                kb.op("dve", lambda: nc.vector.tensor_tensor(out=N["sqg"][0:rows, sl, 0:w], in0=N["g32"][0:rows, sl, 0:w],
                                                             in1=N["g32"][0:rows, sl, 0:w], op=ALU.mult),
                      reads=[N["g32b"][sl]], writes=[N["sqgb"][sl]])
                if ch % gs == gs - 1:
                    finq.append((ch // gs, [(ch - gs + 1 + jj) % 8 for jj in range(gs)]))
                while len(finq) > lag:
                    finish(*finq.pop(0))
        self.stream_tiles(key, ntiles, body)
        while finq:
            finish(*finq.pop(0))
        while pend:
            consumer(*pend.pop(0))

    def store_consumer(self, dst_fn, rows=128):
        kb = self.kb

        def cons(grp, ofn, otb, gs=[None]):
            j = 0
            while True:
                d = dst_fn(grp, j)
                if d is None:
                    break
                kb.dma("act", d, ofn(j), otb, reads=[otb])
                j += 1
        return cons

    def tokmajor_linear(self, key, lhs_fn, lhs_bufs, nsub, dst_fn, T, split=None):
        nc, kb = self.nc, self.kb
        K, Nn, CW, P, k = self.wspec[key]
        KT = K // 128
        cnt = [0]

        def body(t, slot, rb):
            for s in range(nsub):
                bi = cnt[0] % 4
                si = cnt[0] % len(T["tm"])
                cnt[0] += 1
                ps, pb = self.ps[bi], self.psb[bi]

                def mm():
                    ins = None
                    for kt in range(KT):
                        ins = nc.tensor.matmul(ps[:, 0:CW], lhs_fn(kt, s), self.wview(slot, key, kt, 0, CW),
                                               start=(kt == 0), stop=(kt == KT - 1))
                    return ins
                kb.op("pe", mm, reads=[rb] + list(lhs_bufs), writes=[pb])
                tm, tmb = T["tm"][si], T["tmb"][si]
                kb.op("act", lambda: nc.scalar.copy(out=tm[:, 0:CW], in_=ps[:, 0:CW]), reads=[pb], writes=[tmb])
                srcv = tm[:, 0:CW] if split is None else tm[:, 0:CW].rearrange("p (h d) -> p h d", h=split)
                kb.dma("act", dst_fn(s, t), srcv, tmb, reads=[tmb])
        self.stream_tiles(key, Nn // CW, body)

    def tm_arena(self, es):
        nc, kb = self.nc, self.kb
        tg = self.tag("t")
        T = {}
        T["tm"] = [es.enter_context(nc.sbuf_tensor(f"{tg}_tm{t}", [128, 512], BF16)) for t in range(3)]
        T["tmb"] = [kb.dma_buf(f"{tg}tm{t}") for t in range(3)]
        return T

    def resid_linear(self, key, rhs_fn, rhs_bufs, scale, src, t0, xo, xob):
        nc, kb = self.nc, self.kb
        K, Nn, CW, P, k = self.wspec[key]
        KT = K // 128
        TB, NT = self.TB, self.NT

        ntl = Nn // 128

        def xload(t):
            s_ = t % 4
            xsrc = dap(src, (t * 128) * NT + t0, [[NT, 128], [1, TB]])
            kb.dma("sp", xo[s_][:], xsrc, xob[s_], writes=[xob[s_]])
        for t in range(min(2, ntl)):
            xload(t)

        def body(t, slot, rb):
            bi = 4 + (t % 4)
            s = t % 4
            if t + 2 < ntl:
                xload(t + 2)

            def f():
                ins = None
                for kt in range(KT):
                    ins = nc.tensor.matmul(self.ps[bi][:, 0:TB], self.wview(slot, key, kt, 0, 128), rhs_fn(kt),
                                           start=(kt == 0), stop=(kt == KT - 1))
                return ins
            kb.op("pe", f, reads=[rb] + list(rhs_bufs), writes=[self.psb[bi]])
            kb.op("dve", lambda: nc.vector.scalar_tensor_tensor(out=xo[s][:], in0=self.ps[bi][:, 0:TB], scalar=float(scale), in1=xo[s][:],
                                                              op0=ALU.mult, op1=ALU.add),
                  reads=[self.psb[bi]], writes=[xob[s]])
            xdst = dap(self.xr, (t * 128) * NT + t0, [[NT, 128], [1, TB]])
            kb.dma("act", xdst, xo[s][:], xob[s], reads=[xob[s]])
        self.stream_tiles(key, Nn // 128, body)

    def ffn(self, i, which, src):
        nc, kb = self.nc, self.kb
        KD, TB, NB, FF, NT = self.KD, self.TB, self.NB, self.FF, self.NT
        FC = FF // 128
        with ExitStack() as es:
            tg = self.tag("f")
            sb = lambda name, shape, dty: es.enter_context(nc.sbuf_tensor(f"{tg}_{name}", list(shape), dty))
            xn = sb("xn", (128, KD, TB), BF16); xnb = Buf(kb, "xn")
            hT = sb("hT", (128, FC, TB), BF16); hb = [Buf(kb, f"h{f}") for f in range(FC)]
            sg = [sb(f"sg{t}", (128, TB), F32) for t in range(2)]; sgb = [Buf(kb, f"sg{t}") for t in range(2)]
            xo = [sb(f"xo{t}", (128, TB), F32) for t in range(4)]; xob = [kb.dma_buf(f"{tg}xo{t}") for t in range(4)]
            A = self.small_arena(es, TB)
            kgu, kdn = (i, f"gu{which}"), (i, f"dn{which}")
            for blk in range(NB):
                t0 = blk * TB
                srcf = lambda kt: dap(src, (kt * 128) * NT + t0, [[NT, 128], [1, TB]])
                self.norm_from_dram(srcf, KD, TB, i, f"ffn{which}", xn, xnb, A)

                def bodyA(t, slot, rb):
                    bg, bu = (0, 1) if t % 2 == 0 else (2, 3)

                    def mm(c0, bi):
                        def f():
                            ins = None
                            for kt in range(KD):
                                ins = nc.tensor.matmul(self.ps[bi][:, 0:TB], self.wview(slot, kgu, kt, c0, 128), xn[:, kt, :],
                                                       start=(kt == 0), stop=(kt == KD - 1))
                            return ins
                        return f
                    kb.op("pe", mm(0, bg), reads=[rb, xnb], writes=[self.psb[bg]])
                    kb.op("pe", mm(128, bu), reads=[rb, xnb], writes=[self.psb[bu]])
                    q = t % 2
                    kb.op("act", lambda: nc.scalar.activation(out=sg[q][:], in_=self.ps[bg][:, 0:TB], func=AF.Silu),
                          reads=[self.psb[bg]], writes=[sgb[q]])
                    kb.op("dve", lambda: nc.vector.tensor_tensor(out=hT[:, t, :], in0=sg[q][:], in1=self.ps[bu][:, 0:TB], op=ALU.mult),
                          reads=[sgb[q], self.psb[bu]], writes=[hb[t]])
                self.stream_tiles(kgu, FC, bodyA)
                self.resid_linear(kdn, lambda ft: hT[:, ft, :], hb, 0.5, src, t0, xo, xob)
            kb.end_phase()

    def mixer_in(self, i, kd):
        nc, kb = self.nc, self.kb
        KD, TB, NB, NT, H, KVH = self.KD, self.TB, self.NB, self.NT, self.H, self.KVH
        c = self.cfg
        with ExitStack() as es:
            tg = self.tag("m")
            sb = lambda name, shape, dty: es.enter_context(nc.sbuf_tensor(f"{tg}_{name}", list(shape), dty))
            uT = sb("uT", (128, KD, TB), BF16); ub = Buf(kb, "uT")
            A = self.small_arena(es, TB)
            N = self.nl_arena(es, TB, gsmax=8, rope=True)
            cs = sb("cs", (128, 2, NT), F32)
            self.csbuf = kb.dma_buf(tg + "cs")
            kb.dma("sp", cs[:], (self.csa_in if kd == 0 else self.csb_in).ap(), self.csbuf, writes=[self.csbuf])
            MC = self.MW // 128
            gm = self.MHD // 128
            if kd == 0:
                QC = c["QL"] // 128
                KC = c["KVL"] // 128
                cqn = sb("cqn", (128, QC, TB), BF16); cqb = kb.dma_buf(tg + "cqn")
            else:
                T = self.tm_arena(es)
            for blk in range(NB):
                t0 = blk * TB
                srcf = lambda kt: dap(self.xr, (kt * 128) * NT + t0, [[NT, 128], [1, TB]])
                self.norm_from_dram(srcf, KD, TB, i, "mix", uT, ub, A)
                urhs = lambda kt: uT[:, kt, :]
                qm_store = self.store_consumer(lambda g, j: dap(self.QM, ((g * gm + j) * 128) * NT + t0, [[NT, 128], [1, TB]]) if j < gm else None)
                if kd == 0:
                    self.norm_linear((i, "cq"), urhs, [ub], TB, QC, c["QL"], i, "qa", N, None, fixed_out=(cqn, cqb))
                    crhs = lambda kt: cqn[:, kt, :]
                    self.norm_linear((i, "qn"), crhs, [cqb], TB, 1, 128, i, "qnope", N,
                                     self.store_consumer(lambda g, j: dap(self.QS, (g * 128) * NT + t0, [[NT, 128], [1, TB]]) if j < 1 else None),
                                     gain_per_chunk=False)
                    self.norm_linear((i, "qp"), crhs, [cqb], TB, 1, 64, i, "qpe", N,
                                     self.store_consumer(lambda g, j: dap(self.QS, ((H + g) * 128) * NT + t0, [[NT, 128], [1, TB]]) if j < 1 else None),
                                     ones=self.cbf[:, 128:256], rope=cs, t0=t0, gain_per_chunk=False)
                    self.norm_linear((i, "ckv"), urhs, [ub], TB, KC, c["KVL"], i, "kva", N,
                                     self.store_consumer(lambda g, j: dap(self.GL, (j * 128) * NT + t0, [[NT, 128], [1, TB]]) if j < KC else None))
                    self.norm_linear((i, "kpe"), urhs, [ub], TB, 1, 64, i, "kpe", N,
                                     self.store_consumer(lambda g, j: dap(self.GL, (4 * 128) * NT + t0, [[NT, 64], [1, TB]]) if j < 1 else None, rows=64),
                                     rows=64, ones=self.cbf[:, 128:192], rope=cs, t0=t0, gain_per_chunk=False)
                else:
                    self.norm_linear((i, "wq"), urhs, [ub], TB, 1, 128, i, "bq", N,
                                     self.store_consumer(lambda g, j: dap(self.QS, (g * 128) * NT + t0, [[NT, 128], [1, TB]]) if j < 1 else None),
                                     rope=cs, t0=t0, gain_per_chunk=False)
                    self.norm_linear((i, "wk"), urhs, [ub], TB, 1, 128, i, "bk", N,
                                     self.store_consumer(lambda g, j: dap(self.GL, (g * 128) * NT + t0, [[NT, 128], [1, TB]]) if j < 1 else None),
                                     rope=cs, t0=t0, gain_per_chunk=False)
                    W = KVH * 128
                    CWv = self.wspec[(i, "wv")][2]
                    self.tokmajor_linear((i, "wv"), lambda kt, s: uT[:, kt, s * 128:(s + 1) * 128], [ub], TB // 128,
                                         lambda s, t: dap(self.VL, (t0 + s * 128) * W + t * CWv, [[W, 128], [1, CWv]]), T)
                self.norm_linear((i, "qm"), urhs, [ub], TB, gm, self.MHD, i, "memq", N, qm_store)
            kb.end_phase()

    def mla_kv_expand(self, i):
        nc, kb = self.nc, self.kb
        NT, H, TB = self.NT, self.H, self.TB
        c = self.cfg
        KC = c["KVL"] // 128
        NTT = 4 * NT
        with ExitStack() as es:
            tg = self.tag("e")
            sb = lambda name, shape, dty: es.enter_context(nc.sbuf_tensor(f"{tg}_{name}", list(shape), dty))
            ckv = sb("ckv", (128, KC, NTT), BF16); ckb = kb.dma_buf(tg + "ckv")
            N = self.nl_arena(es, TB, gsmax=1)
            T = self.tm_arena(es)
            self.make_lring(es, 6, (c["KVL"] // 128) * self.wspec[(i, "kn")][2])
            kb.wait("sp", self.cc, self.kv_ready)
            first = True
            for ch in range(KC):
                for r in range(4):
                    kb.dma("sp", ckv[:, ch, r * NT:(r + 1) * NT], dap(self.GG, (ch * 512 + r * 128) * NT, [[NT, 128], [1, NT]]), ckb,
                           **(dict(writes=[ckb]) if first else dict(accs=[ckb])))
                    first = False
            for blk in range(NTT // TB):
                t0 = blk * TB
                self.norm_linear((i, "kn"), lambda kt: ckv[:, kt, t0:t0 + TB], [ckb], TB, 1, 128, i, "knope", N,
                                 self.store_consumer(lambda g, j: dap(self.KN, (g * 128) * NTT + t0, [[NTT, 128], [1, TB]]) if j < 1 else None),
                                 gain_per_chunk=False)
            NKC = NTT // 128
            CWv = self.wspec[(i, "vv")][2]
            HGv = CWv // 128
            self.tokmajor_linear((i, "vv"), lambda kt, s: ckv[:, kt, s * 128:(s + 1) * 128], [ckb], NKC,
                                 lambda s, t: dap(self.VA, (t * HGv) * 128 * NKC * 128 + s * 128,
                                                  [[NKC * 128, 128], [128 * NKC * 128, HGv], [1, 128]]), T, split=HGv)
            kb.end_phase()
            self.lring = None

    def mem_kv(self, i):
        nc, kb = self.nc, self.kb
        KD, MT, MW = self.KD, self.MT, self.MW
        gm = self.MHD // 128
        with ExitStack() as es:
            tg = self.tag("k")
            sb = lambda name, shape, dty: es.enter_context(nc.sbuf_tensor(f"{tg}_{name}", list(shape), dty))
            mn = sb("mn", (128, KD, MT), BF16); mnb = Buf(kb, "mn")
            A = self.small_arena(es, MT)
            N = self.nl_arena(es, MT, gsmax=gm)
            T = self.tm_arena(es)
            self.norm_from_dram(lambda kt: dap(self.mem_in, kt * 128 * MT, [[MT, 128], [1, MT]]), KD, MT, i, "memn", mn, mnb, A)
            self.norm_linear((i, "mk"), lambda kt: mn[:, kt, :], [mnb], MT, gm, self.MHD, i, "memk", N,
                             self.store_consumer(lambda g, j: dap(self.MK, ((g * gm + j) * 128) * MT, [[MT, 128], [1, MT]]) if j < gm else None))
            CWv = self.wspec[(i, "mv")][2]
            self.tokmajor_linear((i, "mv"), lambda kt, s: mn[:, kt, s * 128:(s + 1) * 128], [mnb], MT // 128,
                                 lambda s, t: dap(self.MV, (s * 128) * MW + t * CWv, [[MW, 128], [1, CWv]]), T)
            kb.end_phase()

    def attn_core(self, R, kparts, vfn, vbufs, ndv, qparts, qbufs, nkc, scale, w, dst_fn):
        nc, kb = self.nc, self.kb
        ones = self.cbf[:, 0:128]
        obanks = [3, 4]
        if ndv == 1:
            ob = [obanks[R["oi"] % 2]]
            lb = 5 + (R["oi"] % 2)
        else:
            ob = obanks
            lb = 5 + (R["oi"] % 2)
        R["oi"] += 1
        np_ = len(kparts)

        def s_step(kc):
            bi = R["si"] % 3
            R["si"] += 1
            ps, pb = self.ps[bi], self.psb[bi]

            def mm():
                ins = None
                for pi, (kf, rows, kbuf) in enumerate(kparts):
                    ins = nc.tensor.matmul(ps[:, 0:w], kf(kc), qparts[pi], start=(pi == 0), stop=(pi == np_ - 1))
                return ins
            kb.op("pe", mm, reads=[kp[2] for kp in kparts] + list(qbufs), writes=[pb])
            pi_ = R["pi"] % len(R["pt"])
            R["pi"] += 1
            pt, ptb = R["pt"][pi_], R["ptb"][pi_]
            kb.op("act", lambda: nc.scalar.activation(out=pt[:, 0:w], in_=ps[:, 0:w], func=AF.Exp, scale=float(scale)),
                  reads=[pb], writes=[ptb])
            return pt, ptb

        def pv_step(kc, pt, ptb):
            for dv in range(ndv):
                kb.op("pe", lambda: nc.tensor.matmul(self.ps[ob[dv]][:, 0:w], vfn(kc, dv), pt[:, 0:w], start=(kc == 0), stop=(kc == nkc - 1)),
                      reads=[ptb] + list(vbufs), **(dict(writes=[self.psb[ob[dv]]]) if kc == 0 else dict(accs=[self.psb[ob[dv]]])))
            ac, acb = accs2[kc % nacc]
            if kc < nacc:
                kb.op("dve", lambda: nc.vector.tensor_copy(out=ac[:, 0:w], in_=pt[:, 0:w]), reads=[ptb], writes=[acb])
            else:
                kb.op("dve", lambda: nc.vector.tensor_tensor(out=ac[:, 0:w], in0=ac[:, 0:w], in1=pt[:, 0:w], op=ALU.add),
                      reads=[ptb], accs=[acb])
        ai = R["ai"] % 2
        R["ai"] += 1
        nacc = 2 if nkc >= 2 else 1
        accs2 = [(R["acc"][2 * ai + t], R["accb"][2 * ai + t]) for t in range(nacc)]
        LA = 2
        pts = {}
        for kc in range(nkc + LA):
            if kc < nkc:
                pts[kc] = s_step(kc)
            if kc == LA and R["defer"]:
                for f in R["defer"]:
                    f()
                R["defer"] = []
            if kc >= LA:
                pv_step(kc - LA, *pts.pop(kc - LA))

        def lmm():
            ins = None
            for t in range(nacc):
                ins = nc.tensor.matmul(self.ps[lb][:, 0:w], self.onesf[:], accs2[t][0][:, 0:w], start=(t == 0), stop=(t == nacc - 1))
            return ins
        kb.op("pe", lmm, reads=[a[1] for a in accs2] + [self.cbuf], writes=[self.psb[lb]])
        ri = R["ri"] % 2
        R["ri"] += 1
        rl, rlb = R["rl"][ri], R["rlb"][ri]
        kb.op("dve", lambda: nc.vector.reciprocal(out=rl[:, 0:w], in_=self.ps[lb][:, 0:w]), reads=[self.psb[lb]], writes=[rlb])
        for dv in range(ndv):
            oi = R["osi"] % len(R["os"])
            R["osi"] += 1
            os_, osb = R["os"][oi], R["osb"][oi]
            kb.op("dve", lambda: nc.vector.tensor_tensor(out=os_[:, 0:w], in0=self.ps[ob[dv]][:, 0:w], in1=rl[:, 0:w], op=ALU.mult),
                  reads=[self.psb[ob[dv]], rlb], writes=[osb])
            R["defer"].append(lambda d=dst_fn(dv), o=os_, b=osb: kb.dma("act", d, o[:, 0:w], b, reads=[b]))

    def attn_flush(self, R):
        for f in R["defer"]:
            f()
        R["defer"] = []

    def attn_arena(self, es, w):
        nc, kb = self.nc, self.kb
        tg = self.tag("r")
        sb = lambda name, shape, dty: es.enter_context(nc.sbuf_tensor(f"{tg}_{name}", list(shape), dty))
        R = dict(oi=0, si=0, pi=0, ri=0, osi=0, ai=0, defer=[])
        R["acc"] = [sb(f"acc{t}", (128, w), F32) for t in range(4)]
        R["accb"] = [Buf(kb, f"{tg}acc{t}") for t in range(4)]
        R["pt"] = [sb(f"pt{t}", (128, w), BF16) for t in range(5)]
        R["ptb"] = [Buf(kb, f"{tg}pt{t}") for t in range(5)]
        R["rl"] = [sb(f"rl{t}", (128, w), F32) for t in range(2)]
        R["rlb"] = [Buf(kb, f"{tg}rl{t}") for t in range(2)]
        R["os"] = [sb(f"os{t}", (128, w), BF16) for t in range(6)]
        R["osb"] = [kb.dma_buf(f"{tg}os{t}") for t in range(6)]
        return R

    def attention(self, i, kd):
        nc, kb = self.nc, self.kb
        NT, TB, NB, H, KVH = self.NT, self.TB, self.NB, self.H, self.KVH
        NTT = 4 * NT
        NKC = NTT // 128
        with ExitStack() as es:
            tg = self.tag("t")
            sb = lambda name, shape, dty: es.enter_context(nc.sbuf_tensor(f"{tg}_{name}", list(shape), dty))
            R = self.attn_arena(es, TB)
            kt_ = [sb(f"kt{t}", (128, NTT), BF16) for t in range(2)]; ktb = [kb.dma_buf(f"{tg}kt{t}") for t in range(2)]
            vt_ = [sb(f"vt{t}", (128, NKC, 128), BF16) for t in range(2)]; vtb = [kb.dma_buf(f"{tg}vt{t}") for t in range(2)]
            qa = [sb(f"qa{t}", (128, TB), BF16) for t in range(3)]; qab = [kb.dma_buf(f"{tg}qa{t}") for t in range(3)]
            kb.wait("sp", self.cc, self.kv_ready)
            if kd == 0:
                kpe = sb("kpe", (64, NTT), BF16); kpb = kb.dma_buf(tg + "kpe")
                qp = [sb(f"qp{t}", (64, TB), BF16) for t in range(3)]; qpb = [kb.dma_buf(f"{tg}qp{t}") for t in range(3)]
                for r in range(4):
                    kb.dma("sp", kpe[:, r * NT:(r + 1) * NT], dap(self.GG, (4 * 512 + r * 64) * NT, [[NT, 64], [1, NT]]), kpb,
                           **(dict(writes=[kpb]) if r == 0 else dict(accs=[kpb])))
                scale = 192.0 ** -0.5
                W = H * 128
                qi = 0
                for h in range(H):
                    p = h % 2
                    kb.dma("sp", kt_[p][:], dap(self.KN, h * 128 * NTT, [[NTT, 128], [1, NTT]]), ktb[p], writes=[ktb[p]])
                    kb.dma("sp", vt_[p][:], dap(self.VA, h * 128 * NKC * 128, [[NKC * 128, 128], [128, NKC], [1, 128]]), vtb[p], writes=[vtb[p]])
                    for blk in range(NB):
                        t0 = blk * TB
                        s = qi % 3
                        qi += 1
                        kb.dma("sp", qa[s][:], dap(self.QS, h * 128 * NT + t0, [[NT, 128], [1, TB]]), qab[s], writes=[qab[s]])
                        kb.dma("sp", qp[s][:], dap(self.QS, ((H + h // 2) * 128 + (h % 2) * 64) * NT + t0, [[NT, 64], [1, TB]]), qpb[s], writes=[qpb[s]])
                        self.attn_core(R,
                                       [(lambda kc: kt_[p][:, kc * 128:(kc + 1) * 128], 128, ktb[p]),
                                        (lambda kc: kpe[:, kc * 128:(kc + 1) * 128], 64, kpb)],
                                       lambda kc, dv: vt_[p][:, kc, :], [vtb[p]], 1,
                                       [qa[s][:], qp[s][:]], [qab[s], qpb[s]], NKC, scale, TB,
                                       lambda dv: dap(self.OT, h * 128 * NT + t0, [[NT, 128], [1, TB]]))
            else:
                scale = 128.0 ** -0.5
                W = KVH * 128
                G = H // KVH
                rstep = self.v_rstep
                qi = 0
                for kv in range(KVH):
                    p = kv % 2
                    kb.dma("sp", kt_[p][:].rearrange("p (r t) -> p r t", r=4), dap(self.GG, kv * 512 * NT, [[NT, 128], [128 * NT, 4], [1, NT]]), ktb[p], writes=[ktb[p]])
                    first = True
                    for r in range(4):
                        for j in range(NT // rstep):
                            kc0 = r * (NT // 128) + j * (rstep // 128)
                            row0 = j * 4 * rstep + r * rstep
                            kb.dma("sp", vt_[p][:, kc0:kc0 + rstep // 128, :],
                                   dap(self.VG, row0 * W + kv * 128, [[W, 128], [128 * W, rstep // 128], [1, 128]]), vtb[p],
                                   **(dict(writes=[vtb[p]]) if first else dict(accs=[vtb[p]])))
                            first = False
                    for g in range(G):
                        h = kv * G + g
                        for blk in range(NB):
                            t0 = blk * TB
                            s = qi % 3
                            qi += 1
                            kb.dma("sp", qa[s][:], dap(self.QS, h * 128 * NT + t0, [[NT, 128], [1, TB]]), qab[s], writes=[qab[s]])
                            self.attn_core(R, [(lambda kc: kt_[p][:, kc * 128:(kc + 1) * 128], 128, ktb[p])],
                                           lambda kc, dv: vt_[p][:, kc, :], [vtb[p]], 1,
                                           [qa[s][:]], [qab[s]], NKC, scale, TB,
                                           lambda dv: dap(self.OT, h * 128 * NT + t0, [[NT, 128], [1, TB]]))
            gm = self.MHD // 128
            MT, MW = self.MT, self.MW
            mk = sb("mk", (128, self.MW // 128, MT), BF16); mkb = kb.dma_buf(tg + "mk")
            mv = sb("mv", (128, MT // 128, MW), BF16); mvb = kb.dma_buf(tg + "mv")
            kb.dma("sp", mk[:], dap(self.MK, 0, [[MT, 128], [128 * MT, self.MW // 128], [1, MT]]), mkb, writes=[mkb])
            kb.dma("sp", mv[:], dap(self.MV, 0, [[MW, 128], [128 * MW, MT // 128], [1, MW]]), mvb, writes=[mvb])
            qm = [sb(f"qm{t}", (128, gm, TB), BF16) for t in range(2)]; qmb = [kb.dma_buf(f"{tg}qm{t}") for t in range(2)]
            qi = 0
            for mh in range(self.MH):
                for blk in range(NB):
                    t0 = blk * TB
                    s = qi % 2
                    qi += 1
                    kb.dma("sp", qm[s][:], dap(self.QM, (mh * gm * 128) * NT + t0, [[NT, 128], [128 * NT, gm], [1, TB]]), qmb[s], writes=[qmb[s]])
                    kparts = [((lambda kc, d=d: mk[:, mh * gm + d, kc * 128:(kc + 1) * 128]), 128, mkb) for d in range(gm)]
                    self.attn_core(R, kparts, lambda kc, dv: mv[:, kc, mh * self.MHD + dv * 128: mh * self.MHD + (dv + 1) * 128], [mvb], gm,
                                   [qm[s][:, d, :] for d in range(gm)], [qmb[s]], MT // 128, float(self.MHD) ** -0.5, TB,
                                   lambda dv: dap(self.OT, ((H + mh * gm + dv) * 128) * NT + t0, [[NT, 128], [1, TB]]))
            self.attn_flush(R)
            kb.end_phase()

    def out_proj(self, i):
        nc, kb = self.nc, self.kb
        NT, TB, NB, OC = self.NT, self.TB, self.NB, self.OC
        with ExitStack() as es:
            tg = self.tag("o")
            sb = lambda name, shape, dty: es.enter_context(nc.sbuf_tensor(f"{tg}_{name}", list(shape), dty))
            ot = [sb(f"ot{t}", (128, OC, TB), BF16) for t in range(2)]; otb = [kb.dma_buf(f"{tg}ot{t}") for t in range(2)]
            xo = [sb(f"xo{t}", (128, TB), F32) for t in range(4)]; xob = [kb.dma_buf(f"{tg}xo{t}") for t in range(4)]
            for blk in range(NB):
                t0 = blk * TB
                p = blk % 2
                kb.dma("sp", ot[p][:], dap(self.OT, t0, [[NT, 128], [128 * NT, OC], [1, TB]]), otb[p], writes=[otb[p]])
                self.resid_linear((i, "wo"), lambda kt: ot[p][:, kt, :], [otb[p]], 1.0, self.xr, t0, xo, xob)
            kb.end_phase()


def kernel(**inputs):
    cfg = full_cfg()
    pr = Prog(cfg)
    maps = pr.host_inputs(inputs)
    nc = pr.build()
    res = run_bass_kernel_spmd(nc, maps, core_ids=list(range(NCORE)))
    out = np.empty((2, cfg["S"], cfg["D"]), np.float32)
    NT = pr.NT
    for cc in range(NCORE):
        b, q = cc // 4, cc % 4
        o = np.asarray(res.results[cc]["out"]).reshape(cfg["D"], NT)
        out[b, q * NT:(q + 1) * NT, :] = o.T
    return out
```

```python
import numpy as np
import ml_dtypes
from contextlib import ExitStack
import concourse.bass as bass
import concourse.mybir as mybir
from concourse.bass_utils import run_bass_kernel_spmd

F32 = mybir.dt.float32
BF16 = mybir.dt.bfloat16
AF = mybir.ActivationFunctionType
ALU = mybir.AluOpType
EPS = 1e-6
NCORE = 8
G4 = [[0, 1, 2, 3], [4, 5, 6, 7]]
G2 = [[0, 4], [1, 5], [2, 6], [3, 7]]
QOS = "P1"
PMAX = 512 * 1024


def full_cfg():
    return dict(D=4096, FF=6144, S=8192, DEPTH=4, H=24, KVH=6, QL=1024, KVL=512,
                MT=256, MH=4, MHD=256, TB=512, GRID_W=64, THETA=10000.0)


def weight_specs(cfg, kind):
    D, FF, H, KVH, QL, KVL, MH, MHD = (cfg[k] for k in ("D", "FF", "H", "KVH", "QL", "KVL", "MH", "MHD"))
    MW = MH * MHD
    sp = {}
    sp["gu1"] = (D, 2 * FF, 256); sp["dn1"] = (FF, D, 128)
    if kind == 0:
        sp["cq"] = (D, QL, 128); sp["ckv"] = (D, KVL, 128); sp["kpe"] = (D, 64, 64)
        sp["qm"] = (D, MW, 128)
        HG = 4 if H % 4 == 0 else 2
        sp["qn"] = (QL, H * 128, HG * 128); sp["qp"] = (QL, H * 64, 128)
        sp["kn"] = (KVL, H * 128, HG * 128); sp["vv"] = (KVL, H * 128, HG * 128)
    else:
        sp["wq"] = (D, H * 128, 128); sp["wk"] = (D, KVH * 128, 128); sp["wv"] = (D, KVH * 128, 256)
        sp["qm"] = (D, MW, 128)
    sp["mk"] = (D, MW, 128); sp["mv"] = (D, MW, 256)
    sp["wo"] = (H * 128 + MW, D, 128)
    sp["gu2"] = (D, 2 * FF, 256); sp["dn2"] = (FF, D, 128)
    return sp


def piece_size(E):
    per = E // NCORE
    assert per * NCORE == E
    k = -(-per // PMAX)
    while per % k or (per // k) % 512:
        k += 1
    return per // k, k


def block_weight(W, CW):
    K, N = W.shape
    return np.ascontiguousarray(W.reshape(K // 128, 128, N // CW, CW).transpose(2, 1, 0, 3))


def shard_weight(Wb):
    flat = Wb.reshape(-1)
    P, k = piece_size(flat.size)
    u = flat.reshape(k, NCORE, P)
    return [np.ascontiguousarray(u[:, c, :]).reshape(-1, 512) for c in range(NCORE)]


class Sem:
    def __init__(self, h):
        self.h = h
        self.n = 0


class Buf:
    def __init__(self, kb, name, dma=False):
        self.name = name
        self.w = {}
        self.r = {}
        self.dsem = kb.get_dsem(name) if dma else None


class KB:
    def __init__(self, nc, es):
        self.nc = nc
        self.es = es
        self.nsem = 0
        self.dsems = set()
        self.engs = {"pe": nc.tensor, "act": nc.scalar, "dve": nc.vector, "sp": nc.sync, "pool": nc.gpsimd}
        self.esem = {k: self.new_sem("e_" + k) for k in ("pe", "act", "dve")}
        self.known = {k: {} for k in self.engs}
        self.free_dsems = []
        self.phase_dsems = []
        self.in_phase = False

    def new_sem(self, name):
        self.nsem += 1
        return Sem(self.es.enter_context(self.nc.semaphore(f"{name}_{self.nsem}")))

    def get_dsem(self, name):
        if self.free_dsems:
            sm = self.free_dsems.pop()
        else:
            sm = self.new_sem("d_" + name)
            self.dsems.add(sm)
        if self.in_phase:
            self.phase_dsems.append(sm)
        return sm

    def dma_buf(self, name):
        return Buf(self, name, dma=True)

    def end_phase(self):
        self.barrier()
        self.free_dsems.extend(self.phase_dsems)
        self.phase_dsems = []

    def wait(self, eng, sem, val):
        if val <= 0:
            return
        if sem in self.dsems:
            val = sem.n
        kn = self.known[eng]
        if kn.get(sem, 0) >= val:
            return
        self.engs[eng].wait_ge(sem.h, val)
        kn[sem] = val

    def _deps(self, eng, reads, writes, accs):
        for b in list(reads) + list(accs):
            for s, v in b.w.items():
                self.wait(eng, s, v)
        for b in writes:
            for s, v in list(b.w.items()) + list(b.r.items()):
                self.wait(eng, s, v)

    def _commit(self, ev, reads, writes, accs):
        s, v = ev
        for b in reads:
            b.r[s] = v
        for b in writes:
            b.w = {s: v}
            b.r = {}
        for b in accs:
            b.w[s] = v

    def op(self, eng, fn, reads=(), writes=(), accs=()):
        self._deps(eng, reads, writes, accs)
        ins = fn()
        s = self.esem[eng]
        ins.then_inc(s.h, 1)
        s.n += 1
        self._commit((s, s.n), reads, writes, accs)

    def dma(self, eng, out, in_, sb, reads=(), writes=(), accs=(), **kw):
        self._deps(eng, reads, writes, accs)
        ins = self.engs[eng].dma_start(out=out, in_=in_, **kw)
        s = sb.dsem
        ins.then_inc(s.h, 16)
        s.n += 16
        self._commit((s, s.n), reads, writes, accs)

    def barrier(self, engs=("pe", "act", "dve", "sp")):
        for e in engs:
            for s in list(self.esem.values()) + list(self.dsems):
                self.wait(e, s, s.n)


def dap(t, off, dims):
    return bass.AP(t, off, [list(d) for d in dims])


class Prog:
    def __init__(self, cfg):
        self.cfg = cfg
        c = cfg
        self.D, self.FF, self.S, self.TB = c["D"], c["FF"], c["S"], c["TB"]
        self.NT = self.S // 4
        self.NB = self.NT // self.TB
        self.KD = self.D // 128
        self.H, self.KVH = c["H"], c["KVH"]
        self.MH, self.MHD, self.MT = c["MH"], c["MHD"], c["MT"]
        self.MW = self.MH * self.MHD
        self.OC = self.H + self.MW // 128
        self.kinds = [i % 2 for i in range(c["DEPTH"])]
        self.gcol = {}
        n = 0
        for i, kd in enumerate(self.kinds):
            names = [("ffn1", self.KD), ("mix", self.KD), ("memn", self.KD), ("memq", self.MHD // 128),
                     ("memk", self.MHD // 128), ("ffn2", self.KD)]
            if kd == 0:
                names += [("qa", c["QL"] // 128), ("kva", c["KVL"] // 128), ("qnope", 1), ("qpe", 1), ("knope", 1), ("kpe", 1)]
            else:
                names += [("bq", 1), ("bk", 1)]
            for nm, w in names:
                self.gcol[(i, nm)] = n
                n += w
        self.NG = n

    def host_inputs(self, inp):
        c = self.cfg
        D, FF, H, KVH, QL, KVL, MW = self.D, self.FF, self.H, self.KVH, c["QL"], c["KVL"], self.MW
        NT, KD = self.NT, self.KD
        maps = [dict() for _ in range(NCORE)]
        x = np.asarray(inp["x"]); mem = np.asarray(inp["mem"])
        for cc in range(NCORE):
            b, q = cc // 4, cc % 4
            maps[cc]["x"] = np.ascontiguousarray(x[b, q * NT:(q + 1) * NT, :].T).reshape(KD, 128, NT)
            maps[cc]["mem"] = np.ascontiguousarray(mem[b].T).reshape(KD, 128, self.MT)
        gv = np.zeros((128, self.NG), np.float32)

        def put(i, nm, vec):
            col = self.gcol[(i, nm)]
            vec = np.asarray(vec, np.float32)
            if vec.size == 64:
                vec = np.concatenate([vec, vec])
            w = vec.size // 128
            gv[:, col:col + w] = vec.reshape(w, 128).T
        for i, kd in enumerate(self.kinds):
            j = i // 2
            put(i, "ffn1", inp["ffn1_norm"][i]); put(i, "mix", inp["mix_norm"][i]); put(i, "memn", inp["mem_norm"][i])
            put(i, "memq", inp["mem_q_norm"][i]); put(i, "memk", inp["mem_k_norm"][i]); put(i, "ffn2", inp["ffn2_norm"][i])
            if kd == 0:
                put(i, "qa", inp["a_q_a_norm"][j]); put(i, "kva", inp["a_kv_a_norm"][j])
                put(i, "qnope", inp["a_q_nope_norm"][j]); put(i, "qpe", inp["a_q_pe_norm"][j])
                put(i, "knope", inp["a_k_nope_norm"][j]); put(i, "kpe", inp["a_k_pe_norm"][j])
            else:
                put(i, "bq", inp["b_q_norm"][j]); put(i, "bk", inp["b_k_norm"][j])
        cb = np.zeros((128, 256), np.float32)
        cb[:, :128] = 1.0
        cb[:64, 128:192] = 1.0
        cb[64:, 192:256] = 1.0
        perm = np.zeros((128, 128), np.float32)
        for p in range(128):
            perm[p ^ 1, p] = 1.0
        GW = c["GRID_W"]

        def tables(tok0, rot_dim, nrep):
            t = np.arange(tok0, tok0 + NT)
            row = (t // GW).astype(np.float32); col = (t % GW).astype(np.float32)
            ad = rot_dim // 2
            inv = (np.float32(c["THETA"]) ** (-np.arange(0, ad, 2, dtype=np.float32) / np.float32(ad))).astype(np.float32)
            ang = np.concatenate([row[:, None] * inv, col[:, None] * inv], -1).astype(np.float32)
            cos = np.cos(ang).astype(np.float32); sin = np.sin(ang).astype(np.float32)
            C = np.repeat(cos, 2, axis=1)
            Sg = np.repeat(sin, 2, axis=1)
            Sg[:, 0::2] *= -1.0
            C = np.tile(C, (1, nrep)); Sg = np.tile(Sg, (1, nrep))
            return np.ascontiguousarray(np.stack([C.T, Sg.T], axis=1))
        for cc in range(NCORE):
            q = cc % 4
            maps[cc]["gv"] = gv
            maps[cc]["cbf"] = cb.astype(ml_dtypes.bfloat16)
            maps[cc]["perm"] = perm
            maps[cc]["csa"] = tables(q * NT, 64, 2)
            maps[cc]["csb"] = tables(q * NT, 128, 1)
        for i, kd in enumerate(self.kinds):
            j = i // 2
            W = {}
            for t, nm in ((1, "ffn1"), (2, "ffn2")):
                gu = np.asarray(inp[f"{nm}_w_gu"][i])
                g = gu[:, :FF].reshape(D, FF // 128, 128); u = gu[:, FF:].reshape(D, FF // 128, 128)
                W[f"gu{t}"] = np.concatenate([g, u], axis=2).reshape(D, 2 * FF)
                W[f"dn{t}"] = np.asarray(inp[f"{nm}_w_down"][i])
            if kd == 0:
                win = np.asarray(inp["a_w_in"][j])
                W["cq"] = win[:, :QL]; W["ckv"] = win[:, QL:QL + KVL]; W["kpe"] = win[:, QL + KVL:QL + KVL + 64]
                W["qm"] = win[:, QL + KVL + 64:]
                wqb = np.asarray(inp["a_w_q_b"][j]).reshape(QL, H, 192)
                W["qn"] = wqb[:, :, :128].reshape(QL, H * 128); W["qp"] = wqb[:, :, 128:].reshape(QL, H * 64)
                wkv = np.asarray(inp["a_w_kv_b"][j]).reshape(KVL, H, 256)
                W["kn"] = wkv[:, :, :128].reshape(KVL, H * 128); W["vv"] = wkv[:, :, 128:].reshape(KVL, H * 128)
            else:
                win = np.asarray(inp["b_w_in"][j])
                qw, kw = H * 128, KVH * 128
                W["wq"] = win[:, :qw]; W["wk"] = win[:, qw:qw + kw]; W["wv"] = win[:, qw + kw:qw + 2 * kw]
                W["qm"] = win[:, qw + 2 * kw:]
            mkv = np.asarray(inp["w_mem_kv"][i])
            W["mk"] = mkv[:, :MW]; W["mv"] = mkv[:, MW:]
            W["wo"] = np.asarray(inp["w_o"][i])
            for nm, (K, N, CW) in weight_specs(self.cfg, kd).items():
                assert W[nm].shape == (K, N), (nm, W[nm].shape, K, N)
                sh = shard_weight(block_weight(np.ascontiguousarray(W[nm], dtype=np.float32), CW))
                for cc in range(NCORE):
                    maps[cc][f"w{i}_{nm}"] = sh[cc]
        return maps

    def build(self):
        c = self.cfg
        nc = bass.Bass("TRN2", target_bir_lowering=False)
        self.nc = nc
        NT, KD, H, KVH = self.NT, self.KD, self.H, self.KVH
        with ExitStack() as es:
            kb = KB(nc, es)
            self.kb = kb
            dt = lambda name, shape, dty, kind: nc.dram_tensor(name, list(shape), dty, kind=kind)
            self.x_in = dt("x", (KD, 128, NT), F32, "ExternalInput")
            self.mem_in = dt("mem", (KD, 128, self.MT), F32, "ExternalInput")
            self.gv_in = dt("gv", (128, self.NG), F32, "ExternalInput")
            self.cbf_in = dt("cbf", (128, 256), BF16, "ExternalInput")
            self.perm_in = dt("perm", (128, 128), F32, "ExternalInput")
            self.csa_in = dt("csa", (128, 2, NT), F32, "ExternalInput")
            self.csb_in = dt("csb", (128, 2, NT), F32, "ExternalInput")
            self.xr = dt("out", (KD, 128, NT), F32, "ExternalOutput")
            self.wext, self.wg, self.wspec = {}, {}, {}
            for i, kd in enumerate(self.kinds):
                for nm, (K, N, CW) in weight_specs(c, kd).items():
                    E = K * N
                    P, k = piece_size(E)
                    key = (i, nm)
                    self.wext[key] = dt(f"w{i}_{nm}", (k * P // 512, 512), F32, "ExternalInput")
                    self.wg[key] = dt(f"g{i}_{nm}", (E // 512, 512), BF16, "Internal")
                    self.wspec[key] = (K, N, CW, P, k)
            self.castb = [dt(f"castb{t}", (PMAX // 512, 512), BF16, "Internal") for t in range(2)]
            self.pairb = dt("pairb", (2 * PMAX // 512, 512), BF16, "Internal")
            self.OT = dt("OT", (self.OC, 128, NT), BF16, "Internal")
            self.QS = dt("QS", (2 * H, 128, NT), BF16, "Internal")
            self.QM = dt("QM", (self.MW // 128, 128, NT), BF16, "Internal")
            self.GL = dt("GL", (8, 128, NT), BF16, "Internal")
            self.GG = dt("GG", (8, 4 * 128, NT), BF16, "Internal")
            self.VL = dt("VL", (NT, KVH * 128), BF16, "Internal")
            self.VG = dt("VG", (4 * NT, KVH * 128), BF16, "Internal")
            self.KN = dt("KN", (H, 128, 4 * NT), BF16, "Internal")
            self.VA = dt("VA", (4 * NT, H * 128), BF16, "Internal")
            self.MK = dt("MK", (self.MW // 128, 128, self.MT), BF16, "Internal")
            self.MV = dt("MV", (self.MT, self.MW), BF16, "Internal")
            sb = lambda name, shape, dty: es.enter_context(nc.sbuf_tensor(name, list(shape), dty))
            self.NSLOT = c.get("NSLOT", 4)
            self.WS = 8192
            self.ring = sb("ring", (128, self.NSLOT, self.WS), BF16)
            self.ring_b = [kb.dma_buf(f"ring{t}") for t in range(self.NSLOT)]
            self.ring_i = 0
            self.lring = None
            self.gv = sb("gvs", (128, self.NG), F32)
            self.cbf = sb("cbfs", (128, 256), BF16)
            self.perm = sb("perms", (128, 128), F32)
            self.epsc = sb("epsc", (128, 1), F32)
            self.onesf = sb("onesf", (128, 128), F32)
            cb = kb.dma_buf("consts")
            self.cbuf = cb
            kb.dma("sp", self.gv[:], self.gv_in.ap(), cb, writes=[cb])
            kb.dma("sp", self.cbf[:], self.cbf_in.ap(), cb, accs=[cb])
            kb.dma("sp", self.perm[:], self.perm_in.ap(), cb, accs=[cb])
            kb.op("dve", lambda: nc.vector.memset(self.epsc[:], EPS), accs=[cb])
            kb.op("dve", lambda: nc.vector.memset(self.onesf[:], 1.0), accs=[cb])
            self.ps = [es.enter_context(nc.psum_tensor(f"ps{t}", [128, 512], F32)) for t in range(8)]
            self.psb = [Buf(kb, f"ps{t}") for t in range(8)]
            self.cc = kb.new_sem("cc")
            self.castsem = [kb.new_sem("cast0"), kb.new_sem("cast1")]
            self.ph = kb.new_sem("ph")
            self.wready = {}
            self.wunit = {}
            self.unit_i = 0
            self.uid = 0
            kb.barrier()
            kb.in_phase = True

            nl = len(self.kinds)
            self.pool_weights(0)
            import os
            stop = int(os.environ.get("KSTOP", "1000"))
            steps = []
            for i, kd in enumerate(self.kinds):
                src = self.x_in if i == 0 else self.xr
                steps.append(lambda i=i, src=src: self.ffn(i, 1, src))
                steps.append(lambda i=i, kd=kd: self.mixer_in(i, kd))

                def gath(i=i, kd=kd):
                    self.phase_sync_pool()
                    self.pool_kv_gather(i, kd)
                    if i + 1 < nl:
                        self.pool_weights(i + 1)
                steps.append(gath)
                steps.append(lambda i=i: self.mem_kv(i))
                if kd == 0:
                    steps.append(lambda i=i: self.mla_kv_expand(i))
                steps.append(lambda i=i, kd=kd: self.attention(i, kd))
                steps.append(lambda i=i: self.out_proj(i))
                steps.append(lambda i=i: self.ffn(i, 2, self.xr))
            for st in steps[:stop]:
                st()
            kb.barrier()
        return nc

    def tag(self, s):
        self.uid += 1
        return f"{s}{self.uid}"

    def pool_weights(self, i):
        nc = self.nc
        g = nc.gpsimd
        kd = self.kinds[i]
        units = []
        for nm in weight_specs(self.cfg, kd):
            key = (i, nm)
            for u in range(self.wspec[key][4]):
                units.append((key, u))

        def cast(ix):
            key, u = units[ix]
            P = self.wspec[key][3]
            R = P // 512
            t = (self.unit_i + ix) % 2
            cs = self.castsem[t]
            src = dap(self.wext[key], u * P, [[512, R], [1, 512]])
            cbv = dap(self.castb[t], 0, [[512, R], [1, 512]])
            g.dma_start(out=cbv, in_=src).then_inc(cs.h, 16)
            cs.n += 16
            return cs.n
        cast_tgt = {0: cast(0)}
        for ix, (key, u) in enumerate(units):
            if True:
                K, N, CW, P, k = self.wspec[key]
                R = P // 512
                t = (self.unit_i + ix) % 2
                cs = self.castsem[t]
                cbv = dap(self.castb[t], 0, [[512, R], [1, 512]])
                g.wait_ge(cs.h, cast_tgt[ix])
                pv = dap(self.pairb, 0, [[512, 2 * R], [1, 512]])
                g.collective_compute("AllGather", ALU.bypass, replica_groups=G2, ins=[cbv], outs=[pv], dma_qos=QOS).then_inc(self.cc.h, 1)
                self.cc.n += 1
                if ix + 1 < len(units):
                    cast_tgt[ix + 1] = cast(ix + 1)
                g.wait_ge(self.cc.h, self.cc.n)
                for hh in range(2):
                    iv = dap(self.pairb, hh * P, [[512, R], [1, 512]])
                    ov = dap(self.wg[key], u * 8 * P + hh * 4 * P, [[512, 4 * R], [1, 512]])
                    g.collective_compute("AllGather", ALU.bypass, replica_groups=G4, ins=[iv], outs=[ov], dma_qos=QOS).then_inc(self.cc.h, 1)
                    self.cc.n += 1
                g.wait_ge(self.cc.h, self.cc.n)
                self.wunit.setdefault(key, []).append(self.cc.n)
                self.wready[key] = self.cc.n
        self.unit_i += len(units)

    def phase_sync_pool(self):
        kb = self.kb
        kb.barrier()
        self.nc.sync.nop().then_inc(self.ph.h, 1)
        self.ph.n += 1
        self.nc.gpsimd.wait_ge(self.ph.h, self.ph.n)

    def pool_kv_gather(self, i, kd):
        g = self.nc.gpsimd
        NT = self.NT
        assert 128 * NT <= PMAX
        nch = 5 if kd == 0 else self.KVH
        for ch in range(nch):
            rows = 64 if (kd == 0 and ch == 4) else 128
            iv = dap(self.GL, ch * 128 * NT, [[NT, rows], [1, NT]])
            ov = dap(self.GG, ch * 512 * NT, [[NT, 4 * rows], [1, NT]])
            g.collective_compute("AllGather", ALU.bypass, replica_groups=G4, ins=[iv], outs=[ov]).then_inc(self.cc.h, 1)
            self.cc.n += 1
            g.wait_ge(self.cc.h, self.cc.n)
        if kd == 1:
            W = self.KVH * 128
            rstep = 128
            while rstep * 2 * W <= PMAX and NT % (rstep * 2) == 0:
                rstep *= 2
            self.v_rstep = rstep
            for j in range(NT // rstep):
                iv = dap(self.VL, j * rstep * W, [[W, rstep], [1, W]])
                ov = dap(self.VG, j * 4 * rstep * W, [[W, 4 * rstep], [1, W]])
                g.collective_compute("AllGather", ALU.bypass, replica_groups=G4, ins=[iv], outs=[ov]).then_inc(self.cc.h, 1)
                self.cc.n += 1
                g.wait_ge(self.cc.h, self.cc.n)
        self.kv_ready = self.cc.n

    def wtile_load(self, key, t):
        kb = self.kb
        K, N, CW, P, k = self.wspec[key]
        n = (K // 128) * CW
        if self.lring is not None and n <= self.lring["ws"]:
            L = self.lring
            slot = L["i"] % L["n"]
            L["i"] += 1
            b = L["b"][slot]
            u = ((t + 1) * 128 * n - 1) // (8 * P)
            kb.wait("sp", self.cc, self.wunit[key][u])
            src = dap(self.wg[key], t * 128 * n, [[n, 128], [1, n]])
            kb.dma("sp", L["t"][:, slot, 0:n], src, b, writes=[b])
            return ("L", slot)
        assert n <= self.WS
        slot = self.ring_i % self.NSLOT
        self.ring_i += 1
        b = self.ring_b[slot]
        u = ((t + 1) * 128 * n - 1) // (8 * P)
        kb.wait("sp", self.cc, self.wunit[key][u])
        src = dap(self.wg[key], t * 128 * n, [[n, 128], [1, n]])
        kb.dma("sp", self.ring[:, slot, 0:n], src, b, writes=[b])
        return slot

    def wview(self, slot, key, kt, c0, cw):
        CW = self.wspec[key][2]
        if isinstance(slot, tuple):
            return self.lring["t"][:, slot[1], kt * CW + c0: kt * CW + c0 + cw]
        return self.ring[:, slot, kt * CW + c0: kt * CW + c0 + cw]

    def rbuf(self, slot):
        if isinstance(slot, tuple):
            return self.lring["b"][slot[1]]
        return self.ring_b[slot]

    def make_lring(self, es, nslots, ws):
        nc, kb = self.nc, self.kb
        tg = self.tag("lr")
        self.lring = dict(t=es.enter_context(nc.sbuf_tensor(f"{tg}_t", [128, nslots, ws], BF16)),
                          b=[kb.dma_buf(f"{tg}{t}") for t in range(nslots)], i=0, n=nslots, ws=ws)

    def stream_tiles(self, key, ntiles, body):
        K, N, CW, P, k = self.wspec[key]
        n = (K // 128) * CW
        local = self.lring is not None and n <= self.lring["ws"]
        ahead = (self.lring["n"] - 1) if local else (self.NSLOT - 1)
        slots = {}
        for t in range(min(ahead, ntiles)):
            slots[t] = self.wtile_load(key, t)
        for t in range(ntiles):
            body(t, slots[t], self.rbuf(slots[t]))
            if t + ahead < ntiles:
                slots[t + ahead] = self.wtile_load(key, t + ahead)

    def gcolap(self, i, nm, j=0, rows=128):
        col = self.gcol[(i, nm)] + j
        return self.gv[0:rows, col:col + 1]

    def small_arena(self, es, w):
        nc, kb = self.nc, self.kb
        tg = self.tag("a")
        sb = lambda name, shape, dty: es.enter_context(nc.sbuf_tensor(f"{tg}_{name}", list(shape), dty))
        A = {"w": w}
        A["xc"] = [sb(f"xc{t}", (128, w), F32) for t in range(4)]
        A["xcb"] = [kb.dma_buf(f"{tg}xc{t}") for t in range(4)]
        A["sq"] = [sb(f"sq{t}", (128, w), BF16) for t in range(2)]
        A["sqb"] = [Buf(kb, f"{tg}sq{t}") for t in range(2)]
        A["t32"] = sb("t32", (128, w), F32)
        A["t32b"] = Buf(kb, f"{tg}t32")
        A["rstd"] = sb("rstd", (128, w), F32)
        A["rstdb"] = Buf(kb, f"{tg}rstd")
        return A

    def nl_arena(self, es, w, gsmax=8, rope=False):
        nc, kb = self.nc, self.kb
        tg = self.tag("n")
        sb = lambda name, shape, dty: es.enter_context(nc.sbuf_tensor(f"{tg}_{name}", list(shape), dty))
        N = {"w": w, "gsmax": gsmax}
        N["g32"] = sb("g32", (128, 8, w), F32)
        N["g32b"] = [Buf(kb, f"{tg}g{t}") for t in range(8)]
        N["sqg"] = sb("sqg", (128, 8, w), BF16)
        N["sqgb"] = [Buf(kb, f"{tg}q{t}") for t in range(8)]
        N["tmp"] = [sb(f"tmp{t}", (128, w), F32) for t in range(4)]
        N["tmpb"] = [Buf(kb, f"{tg}tmp{t}") for t in range(4)]
        N["rs"] = [sb(f"rs{t}", (128, w), F32) for t in range(4)]
        N["rsb"] = [Buf(kb, f"{tg}rs{t}") for t in range(4)]
        nst = 8 if gsmax == 1 else 3
        N["st"] = [sb(f"st{t}", (128, min(gsmax, 4), w), BF16) for t in range(nst)]
        N["stb"] = [kb.dma_buf(f"{tg}st{t}") for t in range(nst)]
        N["gi"] = 0
        if rope:
            for nm in ("n32", "a32", "b32"):
                N[nm] = [sb(f"{nm}{t}", (128, w), F32) for t in range(3)]
                N[nm + "b"] = [Buf(kb, f"{tg}{nm}{t}") for t in range(3)]
        return N

    def norm_from_dram(self, src_fn, KT, w, i, gname, out_sb, out_b, A):
        for st in self.norm_steps(src_fn, KT, w, i, gname, out_sb, out_b, A):
            st()

    def norm_steps(self, src_fn, KT, w, i, gname, out_sb, out_b, A):
        nc, kb = self.nc, self.kb
        xc, xcb, sq, sqb = A["xc"], A["xcb"], A["sq"], A["sqb"]
        ones = self.cbf[:, 0:128]
        ps, pb = self.ps[6], self.psb[6]
        steps = []

        def p1(kt):
            s = kt % 4
            q = kt % 2
            kb.dma("sp", xc[s][:, 0:w], src_fn(kt), xcb[s], writes=[xcb[s]])
            kb.op("dve", lambda: nc.vector.tensor_tensor(out=sq[q][:, 0:w], in0=xc[s][:, 0:w], in1=xc[s][:, 0:w], op=ALU.mult),
                  reads=[xcb[s]], writes=[sqb[q]])
            kb.op("pe", lambda: nc.tensor.matmul(ps[:, 0:w], ones, sq[q][:, 0:w], start=(kt == 0), stop=(kt == KT - 1)),
                  reads=[sqb[q], self.cbuf], writes=[pb] if kt == 0 else [], accs=[] if kt == 0 else [pb])
            if kt == KT - 1:
                kb.op("act", lambda: nc.scalar.activation(out=A["t32"][:, 0:w], in_=ps[:, 0:w], func=AF.Ln, scale=1.0 / (KT * 128), bias=self.epsc[:, 0:1]),
                      reads=[pb, self.cbuf], writes=[A["t32b"]])
                kb.op("act", lambda: nc.scalar.activation(out=A["rstd"][:, 0:w], in_=A["t32"][:, 0:w], func=AF.Exp, scale=-0.5),
                      reads=[A["t32b"]], writes=[A["rstdb"]])

        def p2(kt):
            s = kt % 4
            kb.dma("sp", xc[s][:, 0:w], src_fn(kt), xcb[s], writes=[xcb[s]])
            kb.op("dve", lambda: nc.vector.scalar_tensor_tensor(out=out_sb[:, kt, 0:w], in0=xc[s][:, 0:w], scalar=self.gcolap(i, gname, kt),
                                                              in1=A["rstd"][:, 0:w], op0=ALU.mult, op1=ALU.mult),
                  reads=[xcb[s], A["rstdb"], self.cbuf], writes=[out_b] if kt == 0 else [], accs=[] if kt == 0 else [out_b])
        for kt in range(KT):
            steps.append(lambda kt=kt: p1(kt))
        for kt in range(KT):
            steps.append(lambda kt=kt: p2(kt))
        return steps

    def norm_from_dram_old(self, src_fn, KT, w, i, gname, out_sb, out_b, A):
        nc, kb = self.nc, self.kb
        xc, xcb, sq, sqb = A["xc"], A["xcb"], A["sq"], A["sqb"]
        ones = self.cbf[:, 0:128]
        ps, pb = self.ps[6], self.psb[6]
        for kt in range(KT):
            s = kt % 4
            q = kt % 2
            kb.dma("sp", xc[s][:, 0:w], src_fn(kt), xcb[s], writes=[xcb[s]])
            kb.op("dve", lambda: nc.vector.tensor_tensor(out=sq[q][:, 0:w], in0=xc[s][:, 0:w], in1=xc[s][:, 0:w], op=ALU.mult),
                  reads=[xcb[s]], writes=[sqb[q]])
            kb.op("pe", lambda: nc.tensor.matmul(ps[:, 0:w], ones, sq[q][:, 0:w], start=(kt == 0), stop=(kt == KT - 1)),
                  reads=[sqb[q], self.cbuf], writes=[pb] if kt == 0 else [], accs=[] if kt == 0 else [pb])
        kb.op("act", lambda: nc.scalar.activation(out=A["t32"][:, 0:w], in_=ps[:, 0:w], func=AF.Ln, scale=1.0 / (KT * 128), bias=self.epsc[:, 0:1]),
              reads=[pb, self.cbuf], writes=[A["t32b"]])
        kb.op("act", lambda: nc.scalar.activation(out=A["rstd"][:, 0:w], in_=A["t32"][:, 0:w], func=AF.Exp, scale=-0.5),
              reads=[A["t32b"]], writes=[A["rstdb"]])
        for kt in range(KT):
            s = kt % 4
            kb.dma("sp", xc[s][:, 0:w], src_fn(kt), xcb[s], writes=[xcb[s]])
            kb.op("dve", lambda: nc.vector.scalar_tensor_tensor(out=out_sb[:, kt, 0:w], in0=xc[s][:, 0:w], scalar=self.gcolap(i, gname, kt),
                                                              in1=A["rstd"][:, 0:w], op0=ALU.mult, op1=ALU.mult),
                  reads=[xcb[s], A["rstdb"], self.cbuf], writes=[out_b] if kt == 0 else [], accs=[] if kt == 0 else [out_b])

    def norm_linear(self, key, rhs_fn, rhs_bufs, w, gs, nfeat, i, gname, N, consumer, rows=128, ones=None,
                    rope=None, t0=0, banks=(0, 1, 2, 3), aux=(4, 5, 6, 7), fixed_out=None, gain_per_chunk=True):
        nc, kb = self.nc, self.kb
        K, Nn, CW, P, k = self.wspec[key]
        KT = K // 128
        ntiles = Nn // CW
        if ones is None:
            ones = self.cbf[:, 0:128]
        cnt = [0]
        auxc = [0]
        pend = []
        finq = []
        lag = 2 if gs == 1 else (1 if gs == 2 else 0)

        def finish(grp, slots):
            par = N["gi"] % 4
            par3 = N["gi"] % 3
            N["gi"] += 1
            bi = aux[auxc[0] % len(aux)]
            auxc[0] += 1
            ps, pb = self.ps[bi], self.psb[bi]

            def mm():
                ins = None
                for jj, sl in enumerate(slots):
                    ins = nc.tensor.matmul(ps[0:rows, 0:w], ones[0:rows, 0:rows], N["sqg"][0:rows, sl, 0:w],
                                           start=(jj == 0), stop=(jj == len(slots) - 1))
                return ins
            kb.op("pe", mm, reads=[N["sqgb"][sl] for sl in slots] + [self.cbuf], writes=[pb])
            tmp, tmpb, rs, rsb = N["tmp"][par], N["tmpb"][par], N["rs"][par], N["rsb"][par]
            kb.op("act", lambda: nc.scalar.activation(out=tmp[0:rows, 0:w], in_=ps[0:rows, 0:w], func=AF.Ln, scale=1.0 / nfeat, bias=self.epsc[0:rows, 0:1]),
                  reads=[pb, self.cbuf], writes=[tmpb])
            kb.op("act", lambda: nc.scalar.activation(out=rs[0:rows, 0:w], in_=tmp[0:rows, 0:w], func=AF.Exp, scale=-0.5),
                  reads=[tmpb], writes=[rsb])
            if fixed_out is not None:
                ot, otb = fixed_out
                ofn = lambda jj: ot[0:rows, grp * gs + jj, 0:w]
            else:
                si = N["gi"] % len(N["st"])
                ot, otb = N["st"][si], N["stb"][si]
                ofn = lambda jj: ot[0:rows, jj, 0:w]
            for jj, sl in enumerate(slots):
                gap = self.gcolap(i, gname, jj if gain_per_chunk else 0, rows)
                g32 = N["g32"][0:rows, sl, 0:w]
                first = (jj == 0) and (fixed_out is None or grp == 0)
                wr = dict(writes=[otb]) if first else dict(accs=[otb])
                if rope is None:
                    kb.op("dve", lambda: nc.vector.scalar_tensor_tensor(out=ofn(jj), in0=g32, scalar=gap, in1=rs[0:rows, 0:w],
                                                                      op0=ALU.mult, op1=ALU.mult),
                          reads=[N["g32b"][sl], rsb, self.cbuf], **wr)
                else:
                    cs = rope
                    n32, n32b = N["n32"][par3], N["n32b"][par3]
                    a32, a32b = N["a32"][par3], N["a32b"][par3]
                    b32, b32b = N["b32"][par3], N["b32b"][par3]
                    kb.op("dve", lambda: nc.vector.scalar_tensor_tensor(out=n32[0:rows, 0:w], in0=g32, scalar=gap, in1=rs[0:rows, 0:w],
                                                                      op0=ALU.mult, op1=ALU.mult),
                          reads=[N["g32b"][sl], rsb, self.cbuf], writes=[n32b])
                    b2 = aux[auxc[0] % len(aux)]
                    auxc[0] += 1
                    ps2, pb2 = self.ps[b2], self.psb[b2]
                    kb.op("pe", lambda: nc.tensor.matmul(ps2[0:rows, 0:w], self.perm[0:rows, 0:rows], n32[0:rows, 0:w], start=True, stop=True),
                          reads=[n32b, self.cbuf], writes=[pb2])
                    kb.op("dve", lambda: nc.vector.tensor_tensor(out=a32[0:rows, 0:w], in0=n32[0:rows, 0:w], in1=cs[0:rows, 0, t0:t0 + w], op=ALU.mult),
                          reads=[n32b, self.csbuf], writes=[a32b])
                    kb.op("dve", lambda: nc.vector.tensor_tensor(out=b32[0:rows, 0:w], in0=ps2[0:rows, 0:w], in1=cs[0:rows, 1, t0:t0 + w], op=ALU.mult),
                          reads=[pb2, self.csbuf], writes=[b32b])
                    kb.op("dve", lambda: nc.vector.tensor_tensor(out=ofn(jj), in0=a32[0:rows, 0:w], in1=b32[0:rows, 0:w], op=ALU.add),
                          reads=[a32b, b32b], **wr)
            if consumer is not None:
                pend.append((grp, ofn, otb))
                while len(pend) > 1:
                    consumer(*pend.pop(0))

        def body(t, slot, rb):
            for c0 in range(0, CW, rows):
                ch = cnt[0]
                cnt[0] += 1
                bi = banks[ch % len(banks)]
                ps, pb = self.ps[bi], self.psb[bi]

                def mm():
                    ins = None
                    for kt in range(KT):
                        ins = nc.tensor.matmul(ps[0:rows, 0:w], self.wview(slot, key, kt, c0, rows), rhs_fn(kt),
                                               start=(kt == 0), stop=(kt == KT - 1))
                    return ins
                kb.op("pe", mm, reads=[rb] + list(rhs_bufs), writes=[pb])
                sl = ch % 8
                kb.op("act", lambda: nc.scalar.copy(out=N["g32"][0:rows, sl, 0:w], in_=ps[0:rows, 0:w]),
                      reads=[pb], writes=[N["g32b"][sl]])
                kb.op("dve", lambda: nc.vector.tensor_tensor(out=N["sqg"][0:rows, sl, 0:w], in0=N["g32"][0:rows, sl, 0:w],
                                                             in1=N["g32"][0:rows, sl, 0:w], op=ALU.mult),
                      reads=[N["g32b"][sl]], writes=[N["sqgb"][sl]])
                if ch % gs == gs - 1:
                    finq.append((ch // gs, [(ch - gs + 1 + jj) % 8 for jj in range(gs)]))
                while len(finq) > lag:
                    finish(*finq.pop(0))
        self.stream_tiles(key, ntiles, body)
        while finq:
            finish(*finq.pop(0))
        while pend:
            consumer(*pend.pop(0))

    def store_consumer(self, dst_fn, rows=128):
        kb = self.kb

        def cons(grp, ofn, otb, gs=[None]):
            j = 0
            while True:
                d = dst_fn(grp, j)
                if d is None:
                    break
                kb.dma("act", d, ofn(j), otb, reads=[otb])
                j += 1
        return cons

    def tokmajor_linear(self, key, lhs_fn, lhs_bufs, nsub, dst_fn, T, split=None):
        nc, kb = self.nc, self.kb
        K, Nn, CW, P, k = self.wspec[key]
        KT = K // 128
        cnt = [0]

        def body(t, slot, rb):
            for s in range(nsub):
                bi = cnt[0] % 4
                si = cnt[0] % len(T["tm"])
                cnt[0] += 1
                ps, pb = self.ps[bi], self.psb[bi]

                def mm():
                    ins = None
                    for kt in range(KT):
                        ins = nc.tensor.matmul(ps[:, 0:CW], lhs_fn(kt, s), self.wview(slot, key, kt, 0, CW),
                                               start=(kt == 0), stop=(kt == KT - 1))
                    return ins
                kb.op("pe", mm, reads=[rb] + list(lhs_bufs), writes=[pb])
                tm, tmb = T["tm"][si], T["tmb"][si]
                kb.op("act", lambda: nc.scalar.copy(out=tm[:, 0:CW], in_=ps[:, 0:CW]), reads=[pb], writes=[tmb])
                srcv = tm[:, 0:CW] if split is None else tm[:, 0:CW].rearrange("p (h d) -> p h d", h=split)
                kb.dma("act", dst_fn(s, t), srcv, tmb, reads=[tmb])
        self.stream_tiles(key, Nn // CW, body)

    def tm_arena(self, es):
        nc, kb = self.nc, self.kb
        tg = self.tag("t")
        T = {}
        T["tm"] = [es.enter_context(nc.sbuf_tensor(f"{tg}_tm{t}", [128, 512], BF16)) for t in range(6)]
        T["tmb"] = [kb.dma_buf(f"{tg}tm{t}") for t in range(6)]
        return T

    def resid_linear(self, key, rhs_fn, rhs_bufs, scale, src, t0, xo, xob, hooks=None):
        nc, kb = self.nc, self.kb
        K, Nn, CW, P, k = self.wspec[key]
        KT = K // 128
        TB, NT = self.TB, self.NT

        ntl = Nn // 128

        def xload(t):
            s_ = t % 4
            xsrc = dap(src, (t * 128) * NT + t0, [[NT, 128], [1, TB]])
            kb.dma("sp", xo[s_][:], xsrc, xob[s_], writes=[xob[s_]])
        for t in range(min(2, ntl)):
            xload(t)

        hooks = list(hooks) if hooks else []
        per = -(-len(hooks) // max(1, ntl - 2)) if hooks else 0

        def body(t, slot, rb):
            bi = (4, 5, 7)[t % 3]
            s = t % 4
            if t + 2 < ntl:
                xload(t + 2)

            def f():
                ins = None
                for kt in range(KT):
                    ins = nc.tensor.matmul(self.ps[bi][:, 0:TB], self.wview(slot, key, kt, 0, 128), rhs_fn(kt),
                                           start=(kt == 0), stop=(kt == KT - 1))
                return ins
            kb.op("pe", f, reads=[rb] + list(rhs_bufs), writes=[self.psb[bi]])
            kb.op("dve", lambda: nc.vector.scalar_tensor_tensor(out=xo[s][:], in0=self.ps[bi][:, 0:TB], scalar=float(scale), in1=xo[s][:],
                                                              op0=ALU.mult, op1=ALU.add),
                  reads=[self.psb[bi]], writes=[xob[s]])
            xdst = dap(self.xr, (t * 128) * NT + t0, [[NT, 128], [1, TB]])
            kb.dma("act", xdst, xo[s][:], xob[s], reads=[xob[s]])
            for _ in range(per):
                if hooks:
                    hooks.pop(0)()
        self.stream_tiles(key, Nn // 128, body)
        while hooks:
            hooks.pop(0)()

    def ffn(self, i, which, src):
        nc, kb = self.nc, self.kb
        KD, TB, NB, FF, NT = self.KD, self.TB, self.NB, self.FF, self.NT
        FC = FF // 128
        with ExitStack() as es:
            tg = self.tag("f")
            sb = lambda name, shape, dty: es.enter_context(nc.sbuf_tensor(f"{tg}_{name}", list(shape), dty))
            xn = sb("xn", (128, KD, TB), BF16); xnb = Buf(kb, "xn")
            hT = sb("hT", (128, FC, TB), BF16); hb = [Buf(kb, f"h{f}") for f in range(FC)]
            sg = [sb(f"sg{t}", (128, TB), F32) for t in range(2)]; sgb = [Buf(kb, f"sg{t}") for t in range(2)]
            xo = [sb(f"xo{t}", (128, TB), F32) for t in range(4)]; xob = [kb.dma_buf(f"{tg}xo{t}") for t in range(4)]
            A = self.small_arena(es, TB)
            kgu, kdn = (i, f"gu{which}"), (i, f"dn{which}")
            for blk in range(NB):
                t0 = blk * TB
                if blk == 0:
                    srcf = lambda kt, t0=t0: dap(src, (kt * 128) * NT + t0, [[NT, 128], [1, TB]])
                    self.norm_from_dram(srcf, KD, TB, i, f"ffn{which}", xn, xnb, A)

                def bodyA(t, slot, rb):
                    bg, bu = (0, 1) if t % 2 == 0 else (2, 3)

                    def mm(c0, bi):
                        def f():
                            ins = None
                            for kt in range(KD):
                                ins = nc.tensor.matmul(self.ps[bi][:, 0:TB], self.wview(slot, kgu, kt, c0, 128), xn[:, kt, :],
                                                       start=(kt == 0), stop=(kt == KD - 1))
                            return ins
                        return f
                    kb.op("pe", mm(0, bg), reads=[rb, xnb], writes=[self.psb[bg]])
                    kb.op("pe", mm(128, bu), reads=[rb, xnb], writes=[self.psb[bu]])
                    q = t % 2
                    kb.op("act", lambda: nc.scalar.activation(out=sg[q][:], in_=self.ps[bg][:, 0:TB], func=AF.Silu),
                          reads=[self.psb[bg]], writes=[sgb[q]])
                    kb.op("dve", lambda: nc.vector.tensor_tensor(out=hT[:, t, :], in0=sg[q][:], in1=self.ps[bu][:, 0:TB], op=ALU.mult),
                          reads=[sgb[q], self.psb[bu]], writes=[hb[t]])
                self.stream_tiles(kgu, FC, bodyA)
                hooks = None
                if blk + 1 < NB:
                    t1 = (blk + 1) * TB
                    srcn = lambda kt, t1=t1: dap(src, (kt * 128) * NT + t1, [[NT, 128], [1, TB]])
                    hooks = self.norm_steps(srcn, KD, TB, i, f"ffn{which}", xn, xnb, A)
                self.resid_linear(kdn, lambda ft: hT[:, ft, :], hb, 0.5, src, t0, xo, xob, hooks=hooks)
            kb.end_phase()

    def mixer_in(self, i, kd):
        nc, kb = self.nc, self.kb
        KD, TB, NB, NT, H, KVH = self.KD, self.TB, self.NB, self.NT, self.H, self.KVH
        c = self.cfg
        with ExitStack() as es:
            tg = self.tag("m")
            sb = lambda name, shape, dty: es.enter_context(nc.sbuf_tensor(f"{tg}_{name}", list(shape), dty))
            uT = sb("uT", (128, KD, TB), BF16); ub = Buf(kb, "uT")
            A = self.small_arena(es, TB)
            N = self.nl_arena(es, TB, gsmax=8, rope=True)
            cs = sb("cs", (128, 2, NT), F32)
            self.csbuf = kb.dma_buf(tg + "cs")
            kb.dma("sp", cs[:], (self.csa_in if kd == 0 else self.csb_in).ap(), self.csbuf, writes=[self.csbuf])
            MC = self.MW // 128
            gm = self.MHD // 128
            if kd == 0:
                QC = c["QL"] // 128
                KC = c["KVL"] // 128
                cqn = sb("cqn", (128, QC, TB), BF16); cqb = kb.dma_buf(tg + "cqn")
            else:
                T = self.tm_arena(es)
            for blk in range(NB):
                t0 = blk * TB
                srcf = lambda kt: dap(self.xr, (kt * 128) * NT + t0, [[NT, 128], [1, TB]])
                self.norm_from_dram(srcf, KD, TB, i, "mix", uT, ub, A)
                urhs = lambda kt: uT[:, kt, :]
                qm_store = self.store_consumer(lambda g, j: dap(self.QM, ((g * gm + j) * 128) * NT + t0, [[NT, 128], [1, TB]]) if j < gm else None)
                if kd == 0:
                    self.norm_linear((i, "cq"), urhs, [ub], TB, QC, c["QL"], i, "qa", N, None, fixed_out=(cqn, cqb))
                    crhs = lambda kt: cqn[:, kt, :]
                    self.norm_linear((i, "qn"), crhs, [cqb], TB, 1, 128, i, "qnope", N,
                                     self.store_consumer(lambda g, j: dap(self.QS, (g * 128) * NT + t0, [[NT, 128], [1, TB]]) if j < 1 else None),
                                     gain_per_chunk=False)
                    self.norm_linear((i, "qp"), crhs, [cqb], TB, 1, 64, i, "qpe", N,
                                     self.store_consumer(lambda g, j: dap(self.QS, ((H + g) * 128) * NT + t0, [[NT, 128], [1, TB]]) if j < 1 else None),
                                     ones=self.cbf[:, 128:256], rope=cs, t0=t0, gain_per_chunk=False)
                    self.norm_linear((i, "ckv"), urhs, [ub], TB, KC, c["KVL"], i, "kva", N,
                                     self.store_consumer(lambda g, j: dap(self.GL, (j * 128) * NT + t0, [[NT, 128], [1, TB]]) if j < KC else None))
                    self.norm_linear((i, "kpe"), urhs, [ub], TB, 1, 64, i, "kpe", N,
                                     self.store_consumer(lambda g, j: dap(self.GL, (4 * 128) * NT + t0, [[NT, 64], [1, TB]]) if j < 1 else None, rows=64),
                                     rows=64, ones=self.cbf[:, 128:192], rope=cs, t0=t0, gain_per_chunk=False)
                else:
                    self.norm_linear((i, "wq"), urhs, [ub], TB, 1, 128, i, "bq", N,
                                     self.store_consumer(lambda g, j: dap(self.QS, (g * 128) * NT + t0, [[NT, 128], [1, TB]]) if j < 1 else None),
                                     rope=cs, t0=t0, gain_per_chunk=False)
                    self.norm_linear((i, "wk"), urhs, [ub], TB, 1, 128, i, "bk", N,
                                     self.store_consumer(lambda g, j: dap(self.GL, (g * 128) * NT + t0, [[NT, 128], [1, TB]]) if j < 1 else None),
                                     rope=cs, t0=t0, gain_per_chunk=False)
                    W = KVH * 128
                    CWv = self.wspec[(i, "wv")][2]
                    self.tokmajor_linear((i, "wv"), lambda kt, s: uT[:, kt, s * 128:(s + 1) * 128], [ub], TB // 128,
                                         lambda s, t: dap(self.VL, (t0 + s * 128) * W + t * CWv, [[W, 128], [1, CWv]]), T)
                self.norm_linear((i, "qm"), urhs, [ub], TB, gm, self.MHD, i, "memq", N, qm_store)
            kb.end_phase()

    def mla_kv_expand(self, i):
        nc, kb = self.nc, self.kb
        NT, H, TB = self.NT, self.H, self.TB
        c = self.cfg
        KC = c["KVL"] // 128
        NTT = 4 * NT
        with ExitStack() as es:
            tg = self.tag("e")
            sb = lambda name, shape, dty: es.enter_context(nc.sbuf_tensor(f"{tg}_{name}", list(shape), dty))
            ckv = sb("ckv", (128, KC, NTT), BF16); ckb = kb.dma_buf(tg + "ckv")
            N = self.nl_arena(es, TB, gsmax=1)
            T = self.tm_arena(es)
            self.make_lring(es, 5, (c["KVL"] // 128) * self.wspec[(i, "kn")][2])
            kb.wait("sp", self.cc, self.kv_ready)
            first = True
            for ch in range(KC):
                for r in range(4):
                    kb.dma("sp", ckv[:, ch, r * NT:(r + 1) * NT], dap(self.GG, (ch * 512 + r * 128) * NT, [[NT, 128], [1, NT]]), ckb,
                           **(dict(writes=[ckb]) if first else dict(accs=[ckb])))
                    first = False
            for blk in range(NTT // TB):
                t0 = blk * TB
                self.norm_linear((i, "kn"), lambda kt: ckv[:, kt, t0:t0 + TB], [ckb], TB, 1, 128, i, "knope", N,
                                 self.store_consumer(lambda g, j: dap(self.KN, (g * 128) * NTT + t0, [[NTT, 128], [1, TB]]) if j < 1 else None),
                                 gain_per_chunk=False)
            NKC = NTT // 128
            CWv = self.wspec[(i, "vv")][2]
            HGv = CWv // 128
            Wv = H * 128
            self.tokmajor_linear((i, "vv"), lambda kt, s: ckv[:, kt, s * 128:(s + 1) * 128], [ckb], NKC,
                                 lambda s, t: dap(self.VA, (s * 128) * Wv + t * CWv, [[Wv, 128], [1, CWv]]), T)
            kb.end_phase()
            self.lring = None

    def mem_kv(self, i):
        nc, kb = self.nc, self.kb
        KD, MT, MW = self.KD, self.MT, self.MW
        gm = self.MHD // 128
        with ExitStack() as es:
            tg = self.tag("k")
            sb = lambda name, shape, dty: es.enter_context(nc.sbuf_tensor(f"{tg}_{name}", list(shape), dty))
            mn = sb("mn", (128, KD, MT), BF16); mnb = Buf(kb, "mn")
            A = self.small_arena(es, MT)
            N = self.nl_arena(es, MT, gsmax=gm)
            T = self.tm_arena(es)
            self.norm_from_dram(lambda kt: dap(self.mem_in, kt * 128 * MT, [[MT, 128], [1, MT]]), KD, MT, i, "memn", mn, mnb, A)
            self.norm_linear((i, "mk"), lambda kt: mn[:, kt, :], [mnb], MT, gm, self.MHD, i, "memk", N,
                             self.store_consumer(lambda g, j: dap(self.MK, ((g * gm + j) * 128) * MT, [[MT, 128], [1, MT]]) if j < gm else None))
            CWv = self.wspec[(i, "mv")][2]
            self.tokmajor_linear((i, "mv"), lambda kt, s: mn[:, kt, s * 128:(s + 1) * 128], [mnb], MT // 128,
                                 lambda s, t: dap(self.MV, (s * 128) * MW + t * CWv, [[MW, 128], [1, CWv]]), T)
            kb.end_phase()

    def attn_core(self, R, kparts, vfn, vbufs, ndv, qparts, qbufs, nkc, scale, w, dst_fn):
        nc, kb = self.nc, self.kb
        ones = self.cbf[:, 0:128]
        obanks = [3, 4]
        if ndv == 1:
            ob = [obanks[R["oi"] % 2]]
            lb = 5 + (R["oi"] % 2)
        else:
            ob = obanks
            lb = 5 + (R["oi"] % 2)
        R["oi"] += 1
        np_ = len(kparts)

        def s_step(kc):
            bi = R["si"] % 3
            R["si"] += 1
            ps, pb = self.ps[bi], self.psb[bi]

            def mm():
                ins = None
                for pi, (kf, rows, kbuf) in enumerate(kparts):
                    ins = nc.tensor.matmul(ps[:, 0:w], kf(kc), qparts[pi], start=(pi == 0), stop=(pi == np_ - 1))
                return ins
            kb.op("pe", mm, reads=[kp[2] for kp in kparts] + list(qbufs), writes=[pb])
            pi_ = R["pi"] % len(R["pt"])
            R["pi"] += 1
            pt, ptb = R["pt"][pi_], R["ptb"][pi_]
            kb.op("act", lambda: nc.scalar.activation(out=pt[:, 0:w], in_=ps[:, 0:w], func=AF.Exp, scale=float(scale)),
                  reads=[pb], writes=[ptb])
            return pt, ptb

        def pv_step(kc, pt, ptb):
            for dv in range(ndv):
                kb.op("pe", lambda: nc.tensor.matmul(self.ps[ob[dv]][:, 0:w], vfn(kc, dv), pt[:, 0:w], start=(kc == 0), stop=(kc == nkc - 1)),
                      reads=[ptb] + list(vbufs), **(dict(writes=[self.psb[ob[dv]]]) if kc == 0 else dict(accs=[self.psb[ob[dv]]])))
            ac, acb = accs2[kc % nacc]
            if kc < nacc:
                kb.op("dve", lambda: nc.vector.tensor_copy(out=ac[:, 0:w], in_=pt[:, 0:w]), reads=[ptb], writes=[acb])
            else:
                kb.op("dve", lambda: nc.vector.tensor_tensor(out=ac[:, 0:w], in0=ac[:, 0:w], in1=pt[:, 0:w], op=ALU.add),
                      reads=[ptb], accs=[acb])
        ai = R["ai"] % 2
        R["ai"] += 1
        nacc = 2 if nkc >= 2 else 1
        accs2 = [(R["acc"][2 * ai + t], R["accb"][2 * ai + t]) for t in range(nacc)]
        LA = 2
        pts = {}
        for kc in range(nkc + LA):
            if kc < nkc:
                pts[kc] = s_step(kc)
            if kc == LA and R["defer"]:
                for f in R["defer"]:
                    f()
                R["defer"] = []
            if kc >= LA:
                pv_step(kc - LA, *pts.pop(kc - LA))

        def lmm():
            ins = None
            for t in range(nacc):
                ins = nc.tensor.matmul(self.ps[lb][:, 0:w], self.onesf[:], accs2[t][0][:, 0:w], start=(t == 0), stop=(t == nacc - 1))
            return ins
        kb.op("pe", lmm, reads=[a[1] for a in accs2] + [self.cbuf], writes=[self.psb[lb]])
        ri = R["ri"] % 2
        R["ri"] += 1
        rl, rlb = R["rl"][ri], R["rlb"][ri]
        kb.op("dve", lambda: nc.vector.reciprocal(out=rl[:, 0:w], in_=self.ps[lb][:, 0:w]), reads=[self.psb[lb]], writes=[rlb])
        for dv in range(ndv):
            oi = R["osi"] % len(R["os"])
            R["osi"] += 1
            os_, osb = R["os"][oi], R["osb"][oi]
            kb.op("dve", lambda: nc.vector.tensor_tensor(out=os_[:, 0:w], in0=self.ps[ob[dv]][:, 0:w], in1=rl[:, 0:w], op=ALU.mult),
                  reads=[self.psb[ob[dv]], rlb], writes=[osb])
            R["defer"].append(lambda d=dst_fn(dv), o=os_, b=osb: kb.dma("act", d, o[:, 0:w], b, reads=[b]))

    def attn_flush(self, R):
        for f in R["defer"]:
            f()
        R["defer"] = []

    def attn_arena(self, es, w):
        nc, kb = self.nc, self.kb
        tg = self.tag("r")
        sb = lambda name, shape, dty: es.enter_context(nc.sbuf_tensor(f"{tg}_{name}", list(shape), dty))
        R = dict(oi=0, si=0, pi=0, ri=0, osi=0, ai=0, defer=[])
        R["acc"] = [sb(f"acc{t}", (128, w), F32) for t in range(4)]
        R["accb"] = [Buf(kb, f"{tg}acc{t}") for t in range(4)]
        R["pt"] = [sb(f"pt{t}", (128, w), BF16) for t in range(5)]
        R["ptb"] = [Buf(kb, f"{tg}pt{t}") for t in range(5)]
        R["rl"] = [sb(f"rl{t}", (128, w), F32) for t in range(2)]
        R["rlb"] = [Buf(kb, f"{tg}rl{t}") for t in range(2)]
        R["os"] = [sb(f"os{t}", (128, w), BF16) for t in range(6)]
        R["osb"] = [kb.dma_buf(f"{tg}os{t}") for t in range(6)]
        return R

    def attention(self, i, kd):
        nc, kb = self.nc, self.kb
        NT, TB, NB, H, KVH = self.NT, self.TB, self.NB, self.H, self.KVH
        NTT = 4 * NT
        NKC = NTT // 128
        with ExitStack() as es:
            tg = self.tag("t")
            sb = lambda name, shape, dty: es.enter_context(nc.sbuf_tensor(f"{tg}_{name}", list(shape), dty))
            R = self.attn_arena(es, TB)
            kt_ = [sb(f"kt{t}", (128, NTT), BF16) for t in range(2)]; ktb = [kb.dma_buf(f"{tg}kt{t}") for t in range(2)]
            vt_ = [sb(f"vt{t}", (128, NKC, 128), BF16) for t in range(2)]; vtb = [kb.dma_buf(f"{tg}vt{t}") for t in range(2)]
            qa = [sb(f"qa{t}", (128, TB), BF16) for t in range(3)]; qab = [kb.dma_buf(f"{tg}qa{t}") for t in range(3)]
            kb.wait("sp", self.cc, self.kv_ready)
            if kd == 0:
                kpe = sb("kpe", (64, NTT), BF16); kpb = kb.dma_buf(tg + "kpe")
                qp = [sb(f"qp{t}", (64, TB), BF16) for t in range(3)]; qpb = [kb.dma_buf(f"{tg}qp{t}") for t in range(3)]
                for r in range(4):
                    kb.dma("sp", kpe[:, r * NT:(r + 1) * NT], dap(self.GG, (4 * 512 + r * 64) * NT, [[NT, 64], [1, NT]]), kpb,
                           **(dict(writes=[kpb]) if r == 0 else dict(accs=[kpb])))
                scale = 192.0 ** -0.5
                W = H * 128
                qi = 0
                for h in range(H):
                    p = h % 2
                    kb.dma("sp", kt_[p][:], dap(self.KN, h * 128 * NTT, [[NTT, 128], [1, NTT]]), ktb[p], writes=[ktb[p]])
                    nq = 4 if NKC % 4 == 0 else 1
                    for qq in range(nq):
                        k0 = qq * (NKC // nq)
                        kb.dma("sp", vt_[p][:, k0:k0 + NKC // nq, :],
                               dap(self.VA, (k0 * 128) * W + h * 128, [[W, 128], [128 * W, NKC // nq], [1, 128]]), vtb[p],
                               **(dict(writes=[vtb[p]]) if qq == 0 else dict(accs=[vtb[p]])))
                    for blk in range(NB):
                        t0 = blk * TB
                        s = qi % 3
                        qi += 1
                        kb.dma("sp", qa[s][:], dap(self.QS, h * 128 * NT + t0, [[NT, 128], [1, TB]]), qab[s], writes=[qab[s]])
                        kb.dma("sp", qp[s][:], dap(self.QS, ((H + h // 2) * 128 + (h % 2) * 64) * NT + t0, [[NT, 64], [1, TB]]), qpb[s], writes=[qpb[s]])
                        self.attn_core(R,
                                       [(lambda kc: kt_[p][:, kc * 128:(kc + 1) * 128], 128, ktb[p]),
                                        (lambda kc: kpe[:, kc * 128:(kc + 1) * 128], 64, kpb)],
                                       lambda kc, dv: vt_[p][:, kc, :], [vtb[p]], 1,
                                       [qa[s][:], qp[s][:]], [qab[s], qpb[s]], NKC, scale, TB,
                                       lambda dv: dap(self.OT, h * 128 * NT + t0, [[NT, 128], [1, TB]]))
            else:
                scale = 128.0 ** -0.5
                W = KVH * 128
                G = H // KVH
                rstep = self.v_rstep
                qi = 0
                for kv in range(KVH):
                    p = kv % 2
                    kb.dma("sp", kt_[p][:].rearrange("p (r t) -> p r t", r=4), dap(self.GG, kv * 512 * NT, [[NT, 128], [128 * NT, 4], [1, NT]]), ktb[p], writes=[ktb[p]])
                    first = True
                    for r in range(4):
                        for j in range(NT // rstep):
                            kc0 = r * (NT // 128) + j * (rstep // 128)
                            row0 = j * 4 * rstep + r * rstep
                            kb.dma("sp", vt_[p][:, kc0:kc0 + rstep // 128, :],
                                   dap(self.VG, row0 * W + kv * 128, [[W, 128], [128 * W, rstep // 128], [1, 128]]), vtb[p],
                                   **(dict(writes=[vtb[p]]) if first else dict(accs=[vtb[p]])))
                            first = False
                    for g in range(G):
                        h = kv * G + g
                        for blk in range(NB):
                            t0 = blk * TB
                            s = qi % 3
                            qi += 1
                            kb.dma("sp", qa[s][:], dap(self.QS, h * 128 * NT + t0, [[NT, 128], [1, TB]]), qab[s], writes=[qab[s]])
                            self.attn_core(R, [(lambda kc: kt_[p][:, kc * 128:(kc + 1) * 128], 128, ktb[p])],
                                           lambda kc, dv: vt_[p][:, kc, :], [vtb[p]], 1,
                                           [qa[s][:]], [qab[s]], NKC, scale, TB,
                                           lambda dv: dap(self.OT, h * 128 * NT + t0, [[NT, 128], [1, TB]]))
            gm = self.MHD // 128
            MT, MW = self.MT, self.MW
            mk = sb("mk", (128, self.MW // 128, MT), BF16); mkb = kb.dma_buf(tg + "mk")
            mv = sb("mv", (128, MT // 128, MW), BF16); mvb = kb.dma_buf(tg + "mv")
            kb.dma("sp", mk[:], dap(self.MK, 0, [[MT, 128], [128 * MT, self.MW // 128], [1, MT]]), mkb, writes=[mkb])
            kb.dma("sp", mv[:], dap(self.MV, 0, [[MW, 128], [128 * MW, MT // 128], [1, MW]]), mvb, writes=[mvb])
            qm = [sb(f"qm{t}", (128, gm, TB), BF16) for t in range(2)]; qmb = [kb.dma_buf(f"{tg}qm{t}") for t in range(2)]
            qi = 0
            for mh in range(self.MH):
                for blk in range(NB):
                    t0 = blk * TB
                    s = qi % 2
                    qi += 1
                    kb.dma("sp", qm[s][:], dap(self.QM, (mh * gm * 128) * NT + t0, [[NT, 128], [128 * NT, gm], [1, TB]]), qmb[s], writes=[qmb[s]])
                    kparts = [((lambda kc, d=d: mk[:, mh * gm + d, kc * 128:(kc + 1) * 128]), 128, mkb) for d in range(gm)]
                    self.attn_core(R, kparts, lambda kc, dv: mv[:, kc, mh * self.MHD + dv * 128: mh * self.MHD + (dv + 1) * 128], [mvb], gm,
                                   [qm[s][:, d, :] for d in range(gm)], [qmb[s]], MT // 128, float(self.MHD) ** -0.5, TB,
                                   lambda dv: dap(self.OT, ((H + mh * gm + dv) * 128) * NT + t0, [[NT, 128], [1, TB]]))
            self.attn_flush(R)
            kb.end_phase()

    def out_proj(self, i):
        nc, kb = self.nc, self.kb
        NT, TB, NB, OC = self.NT, self.TB, self.NB, self.OC
        with ExitStack() as es:
            tg = self.tag("o")
            sb = lambda name, shape, dty: es.enter_context(nc.sbuf_tensor(f"{tg}_{name}", list(shape), dty))
            ot = [sb(f"ot{t}", (128, OC, TB), BF16) for t in range(2)]; otb = [kb.dma_buf(f"{tg}ot{t}") for t in range(2)]
            xo = [sb(f"xo{t}", (128, TB), F32) for t in range(4)]; xob = [kb.dma_buf(f"{tg}xo{t}") for t in range(4)]
            for blk in range(NB):
                t0 = blk * TB
                p = blk % 2
                kb.dma("sp", ot[p][:], dap(self.OT, t0, [[NT, 128], [128 * NT, OC], [1, TB]]), otb[p], writes=[otb[p]])
                self.resid_linear((i, "wo"), lambda kt: ot[p][:, kt, :], [otb[p]], 1.0, self.xr, t0, xo, xob)
            kb.end_phase()


def kernel(**inputs):
    cfg = full_cfg()
    pr = Prog(cfg)
    maps = pr.host_inputs(inputs)
    nc = pr.build()
    res = run_bass_kernel_spmd(nc, maps, core_ids=list(range(NCORE)))
    out = np.empty((2, cfg["S"], cfg["D"]), np.float32)
    NT = pr.NT
    for cc in range(NCORE):
        b, q = cc // 4, cc % 4
        o = np.asarray(res.results[cc]["out"]).reshape(cfg["D"], NT)
        out[b, q * NT:(q + 1) * NT, :] = o.T
    return out
```
